# Optimizing a Trainium2 kernel written in Bass

```python
import jax, jax.numpy as jnp
from jax import lax
import numpy as np


D_MODEL = 1024
BATCH = 8
SEQ = 4096
DEPTH = 4
DEC_BATCH = 8
DEC_SEQ = 8192
PAST_LEN = 128

N_MIXERS = 3
N_SUB = 3
D_FF = 2816
NORM_EPS = 1e-6

MLA_HEADS = 8
MLA_Q_LORA = 384
MLA_KV_LORA = 256
MLA_NOPE = 128
MLA_ROPE = 64
MLA_V = 128
ROPE_THETA = 10000.0
Q_BLOCK = 128
MLA_SCALE = (MLA_NOPE + MLA_ROPE) ** -0.5

GDN_HEADS = 8
GDN_DK = 128
GDN_DV = 128
GDN_QK = GDN_HEADS * GDN_DK
GDN_VW = GDN_HEADS * GDN_DV
GDN_CONV = 5
GDN_CHUNK = 64
GDN_IN = 2 * GDN_QK + 2 * GDN_VW + 4 * GDN_HEADS

FNET_GROUPS = 8

N_LAYERS_A = (DEPTH + 2) // 3
N_LAYERS_B = (DEPTH + 1) // 3
N_LAYERS_C = DEPTH // 3

kernel_name = 'hybrid_mla_gdn_fnet_macaron_encoder'


def _rmsnorm(x, g):
    xf = x.astype(jnp.float32)
    y = xf * lax.rsqrt(jnp.mean(xf * xf, axis=-1, keepdims=True) + NORM_EPS)
    return (y * g.astype(jnp.float32)).astype(x.dtype)


def _l2norm(x):
    return x * lax.rsqrt(jnp.sum(x * x, axis=-1, keepdims=True) + NORM_EPS)


def _modulate(x, mod_j, g_pre):
    h = _rmsnorm(x, g_pre)
    return h * (1.0 + mod_j[:, 1][:, None, :]) + mod_j[:, 0][:, None, :]


def _residual(x, out, mod_j, g_post, weight):
    return x + weight * mod_j[:, 2][:, None, :] * _rmsnorm(out, g_post)


def _swiglu(h, w_in, w_out):
    gu = h @ w_in
    g, u = gu[..., :D_FF], gu[..., D_FF:]
    return (jax.nn.silu(g) * u) @ w_out


def _rotate(x, cos, sin):
    half = x.shape[-1] // 2
    x1, x2 = x[..., :half], x[..., half:]
    return jnp.concatenate([x1 * cos - x2 * sin, x1 * sin + x2 * cos], axis=-1)


def _mla(h, cos, sin, w_down, q_norm, kv_norm, w_uq, w_ukv, w_out):
    B, S, _ = h.shape
    H = MLA_HEADS
    down = h @ w_down
    cq = _rmsnorm(down[..., :MLA_Q_LORA], q_norm)
    ckv = _rmsnorm(down[..., MLA_Q_LORA:MLA_Q_LORA + MLA_KV_LORA], kv_norm)
    k_rope = _rotate(down[..., MLA_Q_LORA + MLA_KV_LORA:], cos, sin)
    q = (cq @ w_uq).reshape(B, S, H, MLA_NOPE + MLA_ROPE)
    q_nope = q[..., :MLA_NOPE]
    q_rope = _rotate(q[..., MLA_NOPE:], cos[:, None, :], sin[:, None, :])
    kv = (ckv @ w_ukv).reshape(B, S, H, MLA_NOPE + MLA_V)
    k_nope, v = kv[..., :MLA_NOPE], kv[..., MLA_NOPE:]
    nb = S // Q_BLOCK

    def to_blocks(t):
        return jnp.moveaxis(t.reshape(B, nb, Q_BLOCK, *t.shape[2:]), 1, 0)

    def attend(args):
        qn_b, qr_b = args
        s = (jnp.einsum('bqhd,bkhd->bhqk', qn_b, k_nope)
             + jnp.einsum('bqhr,bkr->bhqk', qr_b, k_rope))
        p = jax.nn.softmax(s.astype(jnp.float32) * MLA_SCALE, axis=-1).astype(v.dtype)
        return jnp.einsum('bhqk,bkhd->bqhd', p, v)

    o = lax.map(attend, (to_blocks(q_nope), to_blocks(q_rope)))
    o = jnp.moveaxis(o, 0, 1).reshape(B, S, H * MLA_V)
    return o @ w_out


def _centred_conv(x, w):
    K = w.shape[0]
    pad = K // 2
    S = x.shape[1]
    xp = jnp.pad(x, ((0, 0), (pad, pad), (0, 0)))
    return sum(xp[:, t:t + S] * w[t] for t in range(K))


def _delta_chunked(q, k, v, g, beta):
    B, H, S, DK = q.shape
    DV = v.shape[-1]
    C = GDN_CHUNK
    N = S // C
    q = q * (DK ** -0.5)

    def rs(t):
        return t.reshape(B, H, N, C, *t.shape[3:])

    q, k, v, g, beta = rs(q), rs(k), rs(v), rs(g), rs(beta)
    g = jnp.cumsum(g, axis=-1)
    tril = jnp.tril(jnp.ones((C, C), dtype=bool))
    strict = jnp.tril(jnp.ones((C, C), dtype=bool), -1)
    diff = g[..., :, None] - g[..., None, :]
    decay_mat = jnp.where(tril, jnp.exp(jnp.where(tril, diff, 0.0)), 0.0)
    kb = k * beta[..., None]
    m = jnp.where(strict, jnp.einsum('bhnid,bhnjd->bhnij', kb, k) * decay_mat, 0.0)
    a = m + jnp.eye(C, dtype=q.dtype)
    rhs = jnp.concatenate([v * beta[..., None], kb * jnp.exp(g)[..., None]], axis=-1)
    sol = lax.linalg.triangular_solve(a, rhs, left_side=True, lower=True, unit_diagonal=True)
    u, w = sol[..., :DV], sol[..., DV:]
    qk = jnp.einsum('bhnid,bhnjd->bhnij', q, k) * decay_mat
    g_last = g[..., -1]
    k_dec = k * jnp.exp(g_last[..., None] - g)[..., None]
    q_dec = q * jnp.exp(g)[..., None]

    def step(state, xs):
        qd, qkc, uc, wc, kd, gl = xs
        v_new = uc - jnp.einsum('bhcd,bhde->bhce', wc, state)
        o = jnp.einsum('bhcd,bhde->bhce', qd, state) + jnp.einsum('bhij,bhje->bhie', qkc, v_new)
        state = state * jnp.exp(gl)[..., None, None] + jnp.einsum('bhcd,bhce->bhde', kd, v_new)
        return state, o

    xs = tuple(jnp.moveaxis(t, 2, 0) for t in (q_dec, qk, u, w, k_dec, g_last))
    s0 = jnp.zeros((B, H, DK, DV), dtype=q.dtype)
    _, o = lax.scan(step, s0, xs)
    return jnp.moveaxis(o, 0, 2).reshape(B, H, S, DV)


def _gdn(h, w_in, conv_w, a_log, dt_bias, o_norm, w_out):
    B, S, _ = h.shape
    H = GDN_HEADS
    proj = h @ w_in
    n_conv = 2 * GDN_QK + GDN_VW
    qkv = jax.nn.silu(_centred_conv(proj[..., :n_conv], conv_w))
    q = qkv[..., :GDN_QK].reshape(B, S, H, GDN_DK)
    k = qkv[..., GDN_QK:2 * GDN_QK].reshape(B, S, H, GDN_DK)
    v = qkv[..., 2 * GDN_QK:].reshape(B, S, H, GDN_DV)
    off = n_conv
    gate = proj[..., off:off + GDN_VW].reshape(B, S, H, GDN_DV)
    off += GDN_VW
    a = proj[..., off:off + 2 * H].reshape(B, S, 2, H).astype(jnp.float32)
    b = proj[..., off + 2 * H:].reshape(B, S, 2, H).astype(jnp.float32)
    g = -jnp.exp(a_log.astype(jnp.float32)) * jax.nn.softplus(a + dt_bias.astype(jnp.float32))
    beta = jax.nn.sigmoid(b)
    g = jnp.transpose(g, (2, 0, 3, 1))
    beta = jnp.transpose(beta, (2, 0, 3, 1))

    def to_bhs(t):
        return jnp.transpose(t, (0, 2, 1, 3)).astype(jnp.float32)

    q, k, v = _l2norm(to_bhs(q)), _l2norm(to_bhs(k)), to_bhs(v)

    def flip(t):
        return jnp.flip(t, axis=2)

    o_fw = _delta_chunked(q, k, v, g[0], beta[0])
    o_bw = flip(_delta_chunked(flip(q), flip(k), flip(v), flip(g[1]), flip(beta[1])))
    o = jnp.transpose(o_fw + o_bw, (0, 2, 1, 3)).astype(h.dtype)
    o = _rmsnorm(o, o_norm) * jax.nn.silu(gate)
    return o.reshape(B, S, GDN_VW) @ w_out


def _fnet(h, w_out, b_out):
    B, S, D = h.shape
    hg = h.astype(jnp.float32).reshape(B, S, FNET_GROUPS, D // FNET_GROUPS)
    f = jnp.fft.fft2(hg, axes=(1, 3), norm='ortho').real
    return f.reshape(B, S, D).astype(h.dtype) @ w_out + b_out


def _trunk(x, c, prm):
    B, S, D = x.shape
    half = MLA_ROPE // 2
    pos = jnp.arange(S, dtype=jnp.float32)
    inv_freq = ROPE_THETA ** (-jnp.arange(half, dtype=jnp.float32) / half)
    ang = pos[:, None] * inv_freq[None, :]
    cos, sin = jnp.cos(ang).astype(x.dtype), jnp.sin(ang).astype(x.dtype)
    sc = jax.nn.silu(c)
    for l in range(DEPTH):
        i = l // N_MIXERS
        kind = l % N_MIXERS
        mod = (sc @ prm['w_ada'][l] + prm['b_ada'][l]).reshape(B, N_SUB, 3, D)
        h = _modulate(x, mod[:, 0], prm['norm_pre'][l, 0])
        out = _swiglu(h, prm['ffn_w_in'][l, 0], prm['ffn_w_out'][l, 0])
        x = _residual(x, out, mod[:, 0], prm['norm_post'][l, 0], 0.5)
        h = _modulate(x, mod[:, 1], prm['norm_pre'][l, 1])
        if kind == 0:
            out = _mla(h, cos, sin, prm['mla_w_down'][i], prm['mla_q_norm'][i], prm['mla_kv_norm'][i],
                       prm['mla_w_uq'][i], prm['mla_w_ukv'][i], prm['mla_w_out'][i])
        elif kind == 1:
            out = _gdn(h, prm['gdn_w_in'][i], prm['gdn_conv'][i], prm['gdn_a_log'][i],
                       prm['gdn_dt_bias'][i], prm['gdn_o_norm'][i], prm['gdn_w_out'][i])
        else:
            out = _fnet(h, prm['fnet_w_out'][i], prm['fnet_b_out'][i])
        x = _residual(x, out, mod[:, 1], prm['norm_post'][l, 1], 1.0)
        h = _modulate(x, mod[:, 2], prm['norm_pre'][l, 2])
        out = _swiglu(h, prm['ffn_w_in'][l, 1], prm['ffn_w_out'][l, 1])
        x = _residual(x, out, mod[:, 2], prm['norm_post'][l, 2], 0.5)
    return x


def _normal(key, shape, scale):
    return jax.random.normal(key, shape, jnp.float32) * scale


def setup_inputs(seed: int = 0) -> dict:
    key = jax.random.key(seed)
    ks = jax.random.split(key, 26)
    D = D_MODEL
    H = GDN_HEADS
    dt = jnp.exp(jax.random.uniform(ks[18], (N_LAYERS_B, 2, H), jnp.float32,
                                    minval=np.log(1e-3), maxval=np.log(1e-1)))
    return {
        'x_prompt': _normal(ks[0], (BATCH, SEQ, D), 1.0),
        'x_sample': _normal(ks[1], (DEC_BATCH, DEC_SEQ, D), 1.0),
        'c_prompt': _normal(ks[2], (BATCH, D), 1.0),
        'c_sample': _normal(ks[3], (DEC_BATCH, D), 1.0),
        'w_ada': _normal(ks[4], (DEPTH, D, N_SUB * 3 * D), 0.5 * D ** -0.5),
        'b_ada': _normal(ks[5], (DEPTH, N_SUB * 3 * D), 0.02),
        'norm_pre': 1.0 + _normal(ks[6], (DEPTH, N_SUB, D), 0.05),
        'norm_post': 1.0 + _normal(ks[7], (DEPTH, N_SUB, D), 0.05),
        'ffn_w_in': _normal(ks[8], (DEPTH, 2, D, 2 * D_FF), D ** -0.5),
        'ffn_w_out': _normal(ks[9], (DEPTH, 2, D_FF, D), D_FF ** -0.5),
        'mla_w_down': _normal(ks[10], (N_LAYERS_A, D, MLA_Q_LORA + MLA_KV_LORA + MLA_ROPE), D ** -0.5),
        'mla_q_norm': 1.0 + _normal(ks[11], (N_LAYERS_A, MLA_Q_LORA), 0.05),
        'mla_kv_norm': 1.0 + _normal(ks[12], (N_LAYERS_A, MLA_KV_LORA), 0.05),
        'mla_w_uq': _normal(ks[13], (N_LAYERS_A, MLA_Q_LORA, MLA_HEADS * (MLA_NOPE + MLA_ROPE)), MLA_Q_LORA ** -0.5),
        'mla_w_ukv': _normal(ks[14], (N_LAYERS_A, MLA_KV_LORA, MLA_HEADS * (MLA_NOPE + MLA_V)), MLA_KV_LORA ** -0.5),
        'mla_w_out': _normal(ks[15], (N_LAYERS_A, MLA_HEADS * MLA_V, D), (MLA_HEADS * MLA_V) ** -0.5),
        'gdn_w_in': _normal(ks[16], (N_LAYERS_B, D, GDN_IN), D ** -0.5),
        'gdn_conv': _normal(ks[17], (N_LAYERS_B, GDN_CONV, 2 * GDN_QK + GDN_VW), GDN_CONV ** -0.5),
        'gdn_a_log': jnp.log(jax.random.uniform(ks[19], (N_LAYERS_B, 2, H), jnp.float32, minval=1.0, maxval=16.0)),
        'gdn_dt_bias': dt + jnp.log(-jnp.expm1(-dt)),
        'gdn_o_norm': 1.0 + _normal(ks[20], (N_LAYERS_B, GDN_DV), 0.05),
        'gdn_w_out': _normal(ks[21], (N_LAYERS_B, GDN_VW, D), GDN_VW ** -0.5),
        'fnet_w_out': _normal(ks[22], (N_LAYERS_C, D, D), D ** -0.5),
        'fnet_b_out': _normal(ks[23], (N_LAYERS_C, D), 0.02),
    }


def reference(x_prompt, x_sample, c_prompt, c_sample, w_ada, b_ada, norm_pre, norm_post,
              ffn_w_in, ffn_w_out, mla_w_down, mla_q_norm, mla_kv_norm, mla_w_uq, mla_w_ukv,
              mla_w_out, gdn_w_in, gdn_conv, gdn_a_log, gdn_dt_bias, gdn_o_norm, gdn_w_out,
              fnet_w_out, fnet_b_out):
    prm = {
        'w_ada': w_ada, 'b_ada': b_ada, 'norm_pre': norm_pre, 'norm_post': norm_post,
        'ffn_w_in': ffn_w_in, 'ffn_w_out': ffn_w_out,
        'mla_w_down': mla_w_down, 'mla_q_norm': mla_q_norm, 'mla_kv_norm': mla_kv_norm,
        'mla_w_uq': mla_w_uq, 'mla_w_ukv': mla_w_ukv, 'mla_w_out': mla_w_out,
        'gdn_w_in': gdn_w_in, 'gdn_conv': gdn_conv, 'gdn_a_log': gdn_a_log,
        'gdn_dt_bias': gdn_dt_bias, 'gdn_o_norm': gdn_o_norm, 'gdn_w_out': gdn_w_out,
        'fnet_w_out': fnet_w_out, 'fnet_b_out': fnet_b_out,
    }
    y_prompt = _trunk(x_prompt, c_prompt, prm)
    y_sample = _trunk(x_sample, c_sample, prm)
    return (y_prompt, y_sample)
```

```python
import numpy as np
import ml_dtypes
import concourse.bass as bass
import concourse.mybir as mybir
from concourse.ap import AP
from concourse.bass_utils import run_bass_kernel_spmd

F32 = mybir.dt.float32
BF16 = mybir.dt.bfloat16
AF = mybir.ActivationFunctionType
ALU = mybir.AluOpType
AX = mybir.AxisListType

D = 1024
DC = 8
DFF = 2816
FC = 22
DEPTH = 4
EPS = 1e-6
T = 512
N_CORES = 8
DEBUG = False
GDN_STAGE = 9
GDN_SUB = 9
GDN_NLV = 5
GDN_X = 0
LAST = {}


class Res:
    __slots__ = ("name", "w", "r")

    def __init__(self, name=""):
        self.name = name
        self.w = None
        self.r = {}


class Prog:
    ENG = ("pe", "act", "dve", "pool", "sp")

    def __init__(self, nc):
        self.nc = nc
        self.streams = {e: [] for e in self.ENG}
        self.tick = {e: 0 for e in self.ENG}
        self.waited = {e: {} for e in self.ENG}
        self.semnames = list(self.ENG)
        self.dq = {}
        self.dqi = {}
        self.dcount = {}
        for q in ("sp", "pool", "act"):
            self.dq[q] = []
            for i in range(16):
                nm = "d_%s_%d" % (q, i)
                self.semnames.append(nm)
                self.dq[q].append(nm)
                self.dcount[nm] = 0
            self.dqi[q] = 0
        self.last_tok = {}

    def _waits(self, eng, toks):
        out = []
        wd = self.waited[eng]
        for t in toks:
            if t is None:
                continue
            s, v = t
            if wd.get(s, 0) < v:
                wd[s] = v
                out.append((s, v))
        return out

    def _deps(self, eng, reads, writes):
        toks = []
        for r in reads:
            if r.w is not None:
                if r.w[0] == eng and eng == "pe":
                    continue
                toks.append(r.w)
        for w in writes:
            if w.w is not None and w.w[0] != eng:
                toks.append(w.w)
            for s, v in w.r.items():
                if s != eng:
                    toks.append((s, v))
        return toks

    def op(self, eng, fn, reads=(), writes=(), sig=True):
        waits = self._waits(eng, self._deps(eng, reads, writes))
        if sig:
            self.tick[eng] += 1
            tok = (eng, self.tick[eng])
            self.streams[eng].append((waits, fn, (eng, 1)))
        else:
            tok = (eng, self.tick[eng] + 1)
            self.streams[eng].append((waits, fn, None))
        for r in reads:
            if r.r.get(tok[0], 0) < tok[1]:
                r.r[tok[0]] = tok[1]
        for w in writes:
            w.w = tok
            w.r = {}
        self.last_tok[eng] = tok
        return tok

    def dma(self, q, out, in_, reads=(), writes=(), **kw):
        i = self.dqi[q]
        self.dqi[q] = i + 1
        sem = self.dq[q][i % len(self.dq[q])]
        prev = self.dcount[sem]
        toks = self._deps(q, reads, writes)
        if prev:
            toks.append((sem, prev))
        waits = self._waits(q, toks)
        self.dcount[sem] = prev + 16
        tok = (sem, prev + 16)

        def fn(e, out=out, in_=in_, kw=kw):
            return e.dma_start(out=out, in_=in_, **kw)

        self.streams[q].append((waits, fn, (sem, 16)))
        for r in reads:
            if r.r.get(tok[0], 0) < tok[1]:
                r.r[tok[0]] = tok[1]
        for w in writes:
            w.w = tok
            w.r = {}
        return tok

    def barrier(self):
        toks = [(e, self.tick[e]) for e in self.ENG if self.tick[e] > 0]
        toks += [(s, c) for s, c in self.dcount.items() if c > 0]
        for e in self.ENG:
            waits = self._waits(e, toks)
            if waits:
                self.streams[e].append((waits, None, None))

    def emit(self):
        nc = self.nc
        import contextlib
        with contextlib.ExitStack() as es:
            sems = {nm: es.enter_context(nc.semaphore(nm)) for nm in self.semnames}
            block = es.enter_context(nc.Block())
            engobj = {"pe": "tensor", "act": "scalar", "dve": "vector", "pool": "gpsimd", "sp": "sync"}

            def mk(ename):
                stream = self.streams[ename]

                def body(e):
                    for waits, fn, inc in stream:
                        for s, v in waits:
                            e.wait_ge(sems[s], v)
                        if fn is not None:
                            ins = fn(e)
                            if inc is not None:
                                ins.then_inc(sems[inc[0]], inc[1])
                return body

            for ename in self.ENG:
                getattr(block, engobj[ename])(mk(ename))


class Arena:
    def __init__(self, nc, nbytes):
        self.t = nc.alloc_sbuf_tensor("arena", [128, nbytes // 2], BF16)
        self.cap = nbytes
        self.off = 0

    def alloc(self, shape_free, dt, parts=128):
        n = 1
        for s in shape_free:
            n *= s
        esz = 4 if dt == F32 else 2
        nb = (n * esz + 63) // 64 * 64
        assert self.off + nb <= self.cap, ("arena overflow", self.off, nb, self.cap)
        v = self.t[0:parts, self.off // 2:(self.off + n * esz) // 2]
        if dt == F32:
            v = v.bitcast(F32)
        self.off += nb
        if len(shape_free) == 2:
            v = v.rearrange("p (a b) -> p a b", a=shape_free[0])
        elif len(shape_free) == 3:
            v = v.rearrange("p (a b c) -> p a b c", a=shape_free[0], b=shape_free[1])
        return v

    def mark(self):
        return self.off

    def reset(self, m):
        self.off = m


def bcast_free(ap, n, axis_pos=1):
    dims = [list(d) for d in ap.ap]
    dims.insert(axis_pos, [0, n])
    return AP(ap.tensor, ap.offset, dims)


class Ctx:
    pass


def build(S_list, plan, debug_out=None):
    nc = bass.Bass("TRN2", target_bir_lowering=False)
    P = Prog(nc)
    nseq = len(S_list)
    C = Ctx()
    C.nc, C.P, C.S = nc, P, S_list

    C.declared = {}

    def din(name, shape, dt=F32):
        if name not in C.declared:
            C.declared[name] = nc.dram_tensor(name, list(shape), dt, kind="ExternalInput").ap()
        return C.declared[name]

    def dscr(name, shape, dt=F32):
        return nc.dram_tensor(name, list(shape), dt, kind=("ExternalOutput" if DEBUG else "Internal")).ap()

    xin = [din("xT%d" % s, [D, S_list[s]]) for s in range(nseq)]
    yout = [nc.dram_tensor("yT%d" % s, [D, S_list[s]], F32, kind="ExternalOutput").ap() for s in range(nseq)]
    cT = din("cT", [128, DC * nseq])
    layers = sorted(set(it[1] for it in plan)) or [0]
    w_ada = din("w_ada", [len(layers), D, 9 * D])
    b_adaT = din("b_adaT", [128, DEPTH * 72])
    npreT = din("npreT", [128, DEPTH * 3 * DC])
    npostT = din("npostT", [128, DEPTH * 3 * DC])
    C.W = {}
    ar = Arena(nc, 212000)
    C.ar = ar
    ones_bf = ar.alloc([128], BF16)
    modT = ar.alloc([DEPTH * 72 * nseq], F32)
    Apre = ar.alloc([DEPTH * 3 * DC * nseq], F32)
    Gpost = ar.alloc([DEPTH * 3 * DC * nseq], F32)
    npre_sb = ar.alloc([DEPTH * 3 * DC], F32)
    npost_sb = ar.alloc([DEPTH * 3 * DC], F32)
    bada_sb = ar.alloc([DEPTH * 72], F32)
    sc_sb = ar.alloc([DC * nseq], F32)
    R_const = Res("const")
    psum = [nc.alloc_psum_tensor("ps%d" % i, [128, 512], F32) if hasattr(nc, "alloc_psum_tensor") else None for i in range(8)]
    C.psum = [p[:, :] for p in psum]
    C.psr = [Res("ps%d" % i) for i in range(8)]

    P.op("pool", lambda e: e.memset(ones_bf, 1.0 / D), writes=[R_const])
    P.dma("sp", npre_sb, npreT, writes=[R_const])
    P.dma("sp", npost_sb, npostT, writes=[R_const])
    P.dma("sp", bada_sb, b_adaT, writes=[R_const])
    P.dma("sp", sc_sb, cT, writes=[R_const])
    P.op("act", lambda e: e.activation(out=sc_sb, in_=sc_sb, func=AF.Silu), reads=[R_const], writes=[R_const])

    def modidx(l, j, t, dc, s):
        return ((l * 72 + j * 24 + t * 8 + dc) * nseq) + s

    m0 = ar.mark()
    wblk = [ar.alloc([DC, 1024], F32) for _ in range(2)]
    wres = [Res("wada%d" % i) for i in range(2)]
    R_mod = Res("mod")
    sc3 = sc_sb.rearrange("p (k s) -> p k s", s=nseq)
    blk = 0
    for li, l in enumerate(layers):
        pst = C.psum[0]
        for cb in range(9):
            wb, wr = wblk[blk % 2], wres[blk % 2]
            blk += 1
            src = w_ada[li, :, cb * 1024:(cb + 1) * 1024].rearrange("(k p) n -> p k n", p=128)
            for k in range(DC):
                P.dma("sp", wb[:, k, :], src[:, k, :], writes=[wr])
            for o in range(8):
                oc = cb * 8 + o
                for k in range(DC):
                    P.op("pe", lambda e, wb=wb, k=k, o=o, oc=oc, pst=pst: e.matmul(
                        pst[:, oc * nseq:(oc + 1) * nseq], wb[:, k, o * 128:(o + 1) * 128], sc3[:, k, :],
                        start=(k == 0), stop=(k == DC - 1)),
                        reads=[wr, R_const], writes=[C.psr[0]], sig=(k == DC - 1))
        mo = modT[:, l * 72 * nseq:(l + 1) * 72 * nseq].rearrange("p (o s) -> p o s", s=nseq)
        bb = bcast_free(bada_sb[:, l * 72:(l + 1) * 72], nseq, 2)
        P.op("dve", lambda e, mo=mo, bb=bb, pst=pst: e.tensor_tensor(
            mo, pst[:, 0:72 * nseq].rearrange("p (o s) -> p o s", s=nseq), bb, ALU.add),
            reads=[C.psr[0], R_const], writes=[R_mod])
    for l in layers:
        for j in range(3):
            wgt = 1.0 if j == 1 else 0.5
            base = modidx(l, j, 0, 0, 0)
            sh = modT[:, base:base + 8 * nseq]
            scl = modT[:, base + 8 * nseq:base + 16 * nseq].rearrange("p (c s) -> p c s", s=nseq)
            gat = modT[:, base + 16 * nseq:base + 24 * nseq].rearrange("p (c s) -> p c s", s=nseq)
            a_o = Apre[:, (l * 3 + j) * DC * nseq:(l * 3 + j + 1) * DC * nseq].rearrange("p (c s) -> p c s", s=nseq)
            g_o = Gpost[:, (l * 3 + j) * DC * nseq:(l * 3 + j + 1) * DC * nseq].rearrange("p (c s) -> p c s", s=nseq)
            npb = bcast_free(npre_sb[:, (l * 3 + j) * DC:(l * 3 + j + 1) * DC], nseq, 2)
            nqb = bcast_free(npost_sb[:, (l * 3 + j) * DC:(l * 3 + j + 1) * DC], nseq, 2)
            P.op("dve", lambda e, a_o=a_o, scl=scl, npb=npb: e.scalar_tensor_tensor(
                a_o, scl, 1.0, npb, ALU.add, ALU.mult), reads=[R_mod, R_const], writes=[R_mod])
            P.op("dve", lambda e, g_o=g_o, gat=gat, nqb=nqb, wgt=wgt: e.scalar_tensor_tensor(
                g_o, gat, wgt, nqb, ALU.mult, ALU.mult), reads=[R_mod, R_const], writes=[R_mod])
    P.barrier()
    ar.reset(m0)

    def A_col(l, j, c, s):
        i = ((l * 3 + j) * DC + c) * nseq + s
        return Apre[:, i:i + 1]

    def B_col(l, j, c, s):
        i = modidx(l, j, 0, c, s)
        return modT[:, i:i + 1]

    def G_col(l, j, c, s):
        i = ((l * 3 + j) * DC + c) * nseq + s
        return Gpost[:, i:i + 1]

    C.A_col, C.B_col, C.G_col, C.R_mod, C.R_const, C.ones_bf = A_col, B_col, G_col, R_mod, R_const, ones_bf

    C.xsrc = list(xin)
    C.xdst = list(yout)
    C.xres = [[Res("x%d_%d" % (s, t)) for t in range(S_list[s] // T)] for s in range(nseq)]

    def xview(apx, t):
        return apx[:, t * T:(t + 1) * T].rearrange("(c p) t -> p c t", p=128)
    C.xview = xview

    eps_col = ar.alloc([1], F32)
    P.op("pool", lambda e: e.memset(eps_col, EPS), writes=[R_const])

    def rsqrt_ps(dst, dstr, ps, psr, scale=1.0):
        P.op("act", lambda e: e.activation(out=dst, in_=ps, func=AF.Sqrt, bias=eps_col[0:dst.shape[0], :], scale=scale),
             reads=[psr, R_const], writes=[dstr])
        P.op("dve", lambda e: e.reciprocal(dst, dst), reads=[dstr], writes=[dstr])
    C.rsqrt_ps = rsqrt_ps

    def norm_mod(xt, xr, l, j, s, hT, hr, sq, sqr, rstd, rstdr, tmp2, tmpr, ps_i, tw=()):
        ps, psr = C.psum[ps_i], C.psr[ps_i]
        P.op("act", lambda e: e.activation(out=sq, in_=xt, func=AF.Square), reads=[xr], writes=[sqr])
        for c in range(DC):
            P.op("pe", lambda e, c=c: e.matmul(ps, ones_bf, sq[:, c, :], start=(c == 0), stop=(c == DC - 1)),
                 reads=[sqr, R_const], writes=[psr], sig=(c == DC - 1))
        rsqrt_ps(rstd, rstdr, ps, psr)
        for c in range(DC):
            tb, tr = tmp2[c % 2], tmpr[c % 2]
            P.op("dve", lambda e, c=c, tb=tb: e.tensor_tensor(tb, xt[:, c, :], rstd, ALU.mult),
                 reads=[xr, rstdr], writes=[tr] + list(tw))
            P.op("act", lambda e, c=c, tb=tb: e.activation(out=hT[:, c, :], in_=tb, func=AF.Identity,
                                                           scale=A_col(l, j, c, s), bias=B_col(l, j, c, s)),
                 reads=[tr, R_mod], writes=[hr])
    C.norm_mod = norm_mod

    def post_residual(outT, outr, sq, sqr, xt, xr, l, j, s, rstd, rstdr, ps_i):
        ps, psr = C.psum[ps_i], C.psr[ps_i]
        for c in range(DC):
            P.op("pe", lambda e, c=c: e.matmul(ps, ones_bf, sq[:, c, :], start=(c == 0), stop=(c == DC - 1)),
                 reads=[sqr, R_const], writes=[psr], sig=(c == DC - 1))
        rsqrt_ps(rstd, rstdr, ps, psr)
        P.op("pool", lambda e: e.tensor_tensor(outT, outT, bcast_free(rstd, DC, 1), ALU.mult),
             reads=[outr, rstdr], writes=[outr])
        for c in range(DC):
            P.op("dve", lambda e, c=c: e.scalar_tensor_tensor(outT[:, c, :], outT[:, c, :], G_col(l, j, c, s),
                                                              xt[:, c, :], ALU.mult, ALU.add),
                 reads=[outr, xr, R_mod], writes=[outr])
    C.post_residual = post_residual

    def ffn_pass(l, j):
        jj = 0 if j == 0 else 1
        sub = 0 if j == 0 else 2
        ffn_w_in = din("ffn_w_in", [DEPTH, 2, D, 2 * DFF])
        ffn_w_out = din("ffn_w_out", [DEPTH, 2, DFF, D])
        m = ar.mark()
        Win = ar.alloc([DC, 2 * DFF], BF16)
        Wout = ar.alloc([FC, D], BF16)
        xt = ar.alloc([DC, T], F32)
        hT = ar.alloc([DC, T], BF16)
        act = ar.alloc([FC, T], BF16)
        outT = ar.alloc([DC, T], F32)
        rstd = ar.alloc([T], F32)
        rstd2 = rstd
        tmp2 = [outT[:, 0, :], outT[:, 1, :]]
        sg2 = [ar.alloc([T], BF16) for _ in range(2)]
        sq = act[:, 0:DC, :]
        Rw, Rx, Rh, Ract, Rout, Rr = (Res(n) for n in ("w", "xt", "hT", "act", "outT", "rstd"))
        Rr2 = Rr
        Rt = [Res("t0"), Res("t1")]
        Rsg = [Res("sg0"), Res("sg1")]
        wi = ffn_w_in[l, jj].rearrange("(k p) n -> p k n", p=128)
        wo = ffn_w_out[l, jj].rearrange("(f p) n -> p f n", p=128)
        for k in range(DC):
            P.dma("pool", Win[:, k, :], wi[:, k, :], writes=[Rw])
        for f in range(FC):
            P.dma("pool", Wout[:, f, :], wo[:, f, :], writes=[Rw])
        for s in range(nseq):
            for t in range(S_list[s] // T):
                xres = C.xres[s][t]
                P.dma("sp", xt, xview(C.xsrc[s], t), reads=[xres], writes=[Rx])
                norm_mod(xt, Rx, l, sub, s, hT, Rh, sq, Ract, rstd, Rr, tmp2, Rt, 0, tw=[Rout])
                for f in range(FC):
                    pg, pu = 1 + 2 * (f % 2), 2 + 2 * (f % 2)
                    for (pb, col) in ((pg, f), (pu, FC + f)):
                        for k in range(DC):
                            P.op("pe", lambda e, pb=pb, col=col, k=k: e.matmul(
                                C.psum[pb], Win[:, k, col * 128:(col + 1) * 128], hT[:, k, :],
                                start=(k == 0), stop=(k == DC - 1)),
                                reads=[Rw, Rh], writes=[C.psr[pb]], sig=(k == DC - 1))
                    sg, rsg = sg2[f % 2], Rsg[f % 2]
                    P.op("act", lambda e, sg=sg, pg=pg: e.activation(out=sg, in_=C.psum[pg], func=AF.Silu),
                         reads=[C.psr[pg]], writes=[rsg])
                    P.op("dve", lambda e, sg=sg, pu=pu, f=f: e.tensor_tensor(act[:, f, :], sg, C.psum[pu], ALU.mult),
                         reads=[rsg, C.psr[pu]], writes=[Ract])
                for o in range(DC):
                    pb = 5 + (o % 2)
                    for f in range(FC):
                        P.op("pe", lambda e, pb=pb, o=o, f=f: e.matmul(
                            C.psum[pb], Wout[:, f, o * 128:(o + 1) * 128], act[:, f, :],
                            start=(f == 0), stop=(f == FC - 1)),
                            reads=[Rw, Ract], writes=[C.psr[pb]], sig=(f == FC - 1))
                    P.op("act", lambda e, pb=pb, o=o: e.activation(out=outT[:, o, :], in_=C.psum[pb], func=AF.Copy),
                         reads=[C.psr[pb]], writes=[Rout])
                    P.op("dve", lambda e, pb=pb, o=o: e.tensor_tensor(hT[:, o, :], outT[:, o, :], outT[:, o, :], ALU.mult),
                         reads=[Rout], writes=[Rh])
                post_residual(outT, Rout, hT, Rh, xt, Rx, l, sub, s, rstd2, Rr2, 7)
                P.dma("pool", xview(C.xdst[s], t), outT, reads=[Rout], writes=[xres])
            C.xsrc[s] = C.xdst[s]
        P.barrier()
        ar.reset(m)
    C.ffn_pass = ffn_pass


    def mm(ps_ap, psr, pairs, reads):
        n = len(pairs)
        for idx, (l_, r_) in enumerate(pairs):
            P.op("pe", lambda e, l_=l_, r_=r_, idx=idx: e.matmul(ps_ap, l_, r_, start=(idx == 0), stop=(idx == n - 1)),
                 reads=reads, writes=[psr], sig=(idx == n - 1))
    C.mm = mm
    Smax = max(S_list)
    ones_f = ar.alloc([128], F32)
    P.op("pool", lambda e: e.memset(ones_f, 1.0), writes=[R_const])

    def mixer_post(l, wname, wshape, widx, oscr, bias_col=None):
        wdr = din(wname, wshape)
        m = ar.mark()
        Wo = ar.alloc([DC, D], BF16)
        xt = ar.alloc([DC, T], F32)
        ot = ar.alloc([DC, T], BF16)
        sq = ar.alloc([DC, T], BF16)
        outT = ar.alloc([DC, T], F32)
        rstd = ar.alloc([T], F32)
        Rw, Rx, Ro, Rsq, Rout, Rr = (Res(n) for n in ("w", "xt", "ot", "sq", "outT", "rstd"))
        wv = wdr[widx].rearrange("(k p) n -> p k n", p=128)
        for k in range(DC):
            P.dma("pool", Wo[:, k, :], wv[:, k, :], writes=[Rw])
        for s in range(nseq):
            for t in range(S_list[s] // T):
                xres = C.xres[s][t]
                P.dma("sp", xt, xview(C.xsrc[s], t), reads=[xres], writes=[Rx])
                P.dma("sp", ot, xview(oscr[s], t), reads=[C.scr_res], writes=[Ro])
                for o in range(DC):
                    pb = 1 + (o % 2)
                    mm(C.psum[pb], C.psr[pb], [(Wo[:, k, o * 128:(o + 1) * 128], ot[:, k, :]) for k in range(DC)], [Rw, Ro])
                    if bias_col is None:
                        P.op("act", lambda e, pb=pb, o=o: e.activation(out=outT[:, o, :], in_=C.psum[pb], func=AF.Copy),
                             reads=[C.psr[pb]], writes=[Rout])
                    else:
                        P.op("act", lambda e, pb=pb, o=o: e.activation(out=outT[:, o, :], in_=C.psum[pb], func=AF.Identity,
                                                                       bias=bias_col(o), scale=1.0),
                             reads=[C.psr[pb], R_const], writes=[Rout])
                    P.op("dve", lambda e, o=o: e.tensor_tensor(sq[:, o, :], outT[:, o, :], outT[:, o, :], ALU.mult),
                         reads=[Rout], writes=[Rsq])
                post_residual(outT, Rout, sq, Rsq, xt, Rx, l, 1, s, rstd, Rr, 7)
                P.dma("pool", xview(C.xdst[s], t), outT, reads=[Rout], writes=[xres])
            C.xsrc[s] = C.xdst[s]
        P.barrier()
        ar.reset(m)
    C.scr_res = Res("scratch")

    def dbg_dump(name, sb, rlist, dt):
        if not DEBUG:
            return
        shp = list(sb.shape)
        dst = nc.dram_tensor("dbg_" + name, shp, dt, kind="ExternalOutput").ap()
        P.barrier()
        P.dma("sp", dst, sb, reads=rlist)
        P.barrier()
    C.dbg_dump = dbg_dump

    MLA_SCALE = 192.0 ** -0.5

    def mla_layer(l):
        i = l // 3
        wd = din("mla_w_down", [2, D, 704])
        wdsw = din("mla_w_down_sw", [2, D, 64])
        qnT = din("mla_qnT", [128, 2 * 3])
        kvnT = din("mla_kvnT", [128, 2 * 2])
        wuq = din("mla_w_uq", [2, 384, 1536])
        wuqsw = din("mla_w_uq_sw", [2, 384, 512])
        wukv = din("mla_w_ukv", [2, 256, 2048])
        cos2 = din("rope_cos2", [64, Smax])
        sin2s = din("rope_sin2s", [64, Smax])
        oscr = [dscr("mla_o%d_%d" % (l, s), [D, S_list[s]], BF16) for s in range(nseq)]
        def seq_body(s):
            S = S_list[s]
            NT = S // T
            m = ar.mark()
            cqT = ar.alloc([3, S], BF16)
            ckvT = ar.alloc([2, S], BF16)
            krT = ar.alloc([S], BF16)
            Rcq, Rckv, Rkr = Res("cq"), Res("ckv"), Res("kr")
            m1 = ar.mark()
            def pre():
                Wd = ar.alloc([DC, 704], BF16)
                Wdsw = ar.alloc([DC, 64], BF16)
                nrm = ar.alloc([8], F32)
                xt = ar.alloc([DC, T], F32)
                hT = ar.alloc([DC, T], BF16)
                sq = ar.alloc([DC, T], BF16)
                rstd = ar.alloc([T], F32)
                tmp2 = [ar.alloc([T], F32) for _ in range(2)]
                dn = ar.alloc([3, T], F32)
                dsq = ar.alloc([3, T], BF16)
                rs2 = ar.alloc([T], F32)
                cs = ar.alloc([2, T], F32)
                rt = ar.alloc([2, T], F32)
                Rw, Rx, Rh, Rsq, Rr, Rdn, Rdsq, Rrs2, Rcs, Rrt = (Res(n) for n in "w x h sq r dn dsq rs2 cs rt".split())
                Rt = [Res("t0"), Res("t1")]
                wv = wd[i].rearrange("(k p) n -> p k n", p=128)
                wsv = wdsw[i].rearrange("(k p) n -> p k n", p=128)
                for k in range(DC):
                    P.dma("pool", Wd[:, k, :], wv[:, k, :], writes=[Rw])
                    P.dma("pool", Wdsw[:, k, :], wsv[:, k, :], writes=[Rw])
                P.dma("sp", nrm[:, 0:3], qnT[:, i * 3:(i + 1) * 3], writes=[Rw])
                P.dma("sp", nrm[:, 3:5], kvnT[:, i * 2:(i + 1) * 2], writes=[Rw])
                for t in range(NT):
                    tsl = slice(t * T, (t + 1) * T)
                    P.dma("sp", xt, xview(C.xsrc[s], t), reads=[C.xres[s][t]], writes=[Rx])
                    P.dma("sp", cs[0:64, 0, :], cos2[:, tsl], writes=[Rcs])
                    P.dma("sp", cs[0:64, 1, :], sin2s[:, tsl], writes=[Rcs])
                    norm_mod(xt, Rx, l, 1, s, hT, Rh, sq, Rsq, rstd, Rr, tmp2, Rt, 0)
                    for (nch, col0, dst, rdst, ncol0, fan) in ((3, 0, cqT, Rcq, 0, 384.0), (2, 384, ckvT, Rckv, 3, 256.0)):
                        for c in range(nch):
                            pb = 1 + (c % 2)
                            mm(C.psum[pb], C.psr[pb], [(Wd[:, k, col0 + c * 128:col0 + (c + 1) * 128], hT[:, k, :]) for k in range(DC)], [Rw, Rh])
                            P.op("act", lambda e, pb=pb, c=c: e.activation(out=dn[:, c, :], in_=C.psum[pb], func=AF.Copy),
                                 reads=[C.psr[pb]], writes=[Rdn])
                            P.op("dve", lambda e, c=c: e.tensor_tensor(dsq[:, c, :], dn[:, c, :], dn[:, c, :], ALU.mult),
                                 reads=[Rdn], writes=[Rdsq])
                        mm(C.psum[3], C.psr[3], [(ones_bf, dsq[:, c, :]) for c in range(nch)], [Rdsq, R_const])
                        rsqrt_ps(rs2, Rrs2, C.psum[3], C.psr[3], scale=D / fan)
                        if t == 0 and s == 0:
                            dbg_dump("dn%d" % nch, dn, [Rdn], F32)
                            dbg_dump("rs2%d" % nch, rs2, [Rrs2], F32)
                            dbg_dump("nrm%d" % nch, nrm, [Rw], F32)
                            dbg_dump("hT%d" % nch, hT, [Rh], BF16)
                        for c in range(nch):
                            P.op("dve", lambda e, c=c, dst=dst, ncol0=ncol0, tsl=tsl: e.scalar_tensor_tensor(
                                dst[:, c, tsl], dn[:, c, :], nrm[:, ncol0 + c:ncol0 + c + 1], rs2, ALU.mult, ALU.mult),
                                reads=[Rdn, Rrs2, Rw], writes=[rdst])
                    mm(C.psum[4][0:64, :], C.psr[4], [(Wd[:, k, 640:704], hT[:, k, :]) for k in range(DC)], [Rw, Rh])
                    mm(C.psum[5][0:64, :], C.psr[5], [(Wdsw[:, k, :], hT[:, k, :]) for k in range(DC)], [Rw, Rh])
                    P.op("dve", lambda e: e.tensor_tensor(rt[0:64, 0, :], C.psum[4][0:64, :], cs[0:64, 0, :], ALU.mult),
                         reads=[C.psr[4], Rcs], writes=[Rrt])
                    P.op("dve", lambda e: e.tensor_tensor(rt[0:64, 1, :], C.psum[5][0:64, :], cs[0:64, 1, :], ALU.mult),
                         reads=[C.psr[5], Rcs], writes=[Rrt])
                    P.op("pool", lambda e, tsl=tsl: e.tensor_tensor(krT[0:64, tsl], rt[0:64, 0, :], rt[0:64, 1, :], ALU.add),
                         reads=[Rrt], writes=[Rkr])
                    if t == 0 and s == 0:
                        dbg_dump("rt", rt, [Rrt], F32)
                        dbg_dump("cs", cs, [Rcs], F32)
                if DEBUG:
                    C.dbgR = dict(Rdn=None)
            pre()
            dbg_dump("cqT%d" % s, cqT, [Rcq], BF16)
            dbg_dump("ckvT%d" % s, ckvT, [Rckv], BF16)
            dbg_dump("krT%d" % s, krT[0:64, :], [Rkr], BF16)
            P.barrier()
            ar.reset(m1)
            def attn():
                Wuq = ar.alloc([3, 1536], BF16)
                Wuqs = ar.alloc([3, 512], BF16)
                Wukv = ar.alloc([2, 2048], BF16)
                KT = ar.alloc([S], BF16)
                QT = ar.alloc([S], BF16)
                qrT = ar.alloc([S], BF16)
                Vtm = ar.alloc([S // 128, 128], BF16)
                PT = [ar.alloc([T], BF16) for _ in range(3)]
                Pacc = ar.alloc([T], F32)
                rcp = ar.alloc([T], F32)
                osb = [ar.alloc([T], BF16) for _ in range(2)]
                csq = ar.alloc([2, S], F32) if False else None
                cs = ar.alloc([2, T], F32)
                rt = ar.alloc([2, T], F32)
                Rw, RK, RQ, Rqr, RV, Racc, Rrcp, Rcs, Rrt = (Res(n) for n in "w K Q qr V acc rcp cs rt".split())
                RPT = [Res("pt%d" % k) for k in range(3)]
                Ros = [Res("os0"), Res("os1")]
                for (dst, srcw, nk) in ((Wuq, wuq, 3), (Wuqs, wuqsw, 3), (Wukv, wukv, 2)):
                    v = srcw[i].rearrange("(k p) n -> p k n", p=128)
                    for k in range(nk):
                        P.dma("pool", dst[:, k, :], v[:, k, :], writes=[Rw])
                ocnt = 0
                for hd in range(8):
                    for t in range(NT):
                        tsl = slice(t * T, (t + 1) * T)
                        mm(C.psum[0], C.psr[0], [(Wukv[:, k, hd * 256:hd * 256 + 128], ckvT[:, k, tsl]) for k in range(2)], [Rw, Rckv])
                        P.op("act", lambda e, tsl=tsl: e.activation(out=KT[:, tsl], in_=C.psum[0], func=AF.Copy),
                             reads=[C.psr[0]], writes=[RK])
                        mm(C.psum[1], C.psr[1], [(Wuq[:, k, hd * 192:hd * 192 + 128], cqT[:, k, tsl]) for k in range(3)], [Rw, Rcq])
                        P.op("dve", lambda e, tsl=tsl: e.tensor_scalar(QT[:, tsl], C.psum[1], MLA_SCALE, None, ALU.mult),
                             reads=[C.psr[1]], writes=[RQ])
                        mm(C.psum[2][0:64, :], C.psr[2], [(Wuq[:, k, hd * 192 + 128:hd * 192 + 192], cqT[:, k, tsl]) for k in range(3)], [Rw, Rcq])
                        mm(C.psum[3][0:64, :], C.psr[3], [(Wuqs[:, k, hd * 64:hd * 64 + 64], cqT[:, k, tsl]) for k in range(3)], [Rw, Rcq])
                        P.dma("sp", cs[0:64, 0, :], cos2[:, tsl], writes=[Rcs])
                        P.dma("sp", cs[0:64, 1, :], sin2s[:, tsl], writes=[Rcs])
                        P.op("dve", lambda e: e.scalar_tensor_tensor(rt[0:64, 0, :], C.psum[2][0:64, :], MLA_SCALE, cs[0:64, 0, :], ALU.mult, ALU.mult),
                             reads=[C.psr[2], Rcs], writes=[Rrt])
                        P.op("dve", lambda e: e.scalar_tensor_tensor(rt[0:64, 1, :], C.psum[3][0:64, :], MLA_SCALE, cs[0:64, 1, :], ALU.mult, ALU.mult),
                             reads=[C.psr[3], Rcs], writes=[Rrt])
                        P.op("pool", lambda e, tsl=tsl: e.tensor_tensor(qrT[0:64, tsl], rt[0:64, 0, :], rt[0:64, 1, :], ALU.add),
                             reads=[Rrt], writes=[Rqr])
                        for q4 in range(4):
                            tt = t * 4 + q4
                            mm(C.psum[4][:, q4 * 128:(q4 + 1) * 128], C.psr[4],
                               [(ckvT[:, k, tt * 128:(tt + 1) * 128], Wukv[:, k, hd * 256 + 128:hd * 256 + 256]) for k in range(2)], [Rw, Rckv])
                        P.op("act", lambda e, t=t: e.activation(out=Vtm[:, t * 4:(t + 1) * 4, :], in_=C.psum[4].rearrange("p (a b) -> p a b", a=4), func=AF.Copy),
                             reads=[C.psr[4]], writes=[RV])
                    if hd == 0:
                        dbg_dump("KT%d" % s, KT, [RK], BF16)
                        dbg_dump("QT%d" % s, QT, [RQ], BF16)
                        dbg_dump("qrT%d" % s, qrT[0:64, :], [Rqr], BF16)
                        dbg_dump("Vtm%d" % s, Vtm, [RV], BF16)
                    nkt = S // 128
                    for qt in range(NT):
                        qsl = slice(qt * T, (qt + 1) * T)
                        for kt in range(nkt):
                            ksl = slice(kt * 128, (kt + 1) * 128)
                            pb = kt % 3
                            mm(C.psum[pb], C.psr[pb], [(KT[:, ksl], QT[:, qsl]), (krT[0:64, ksl], qrT[0:64, qsl])], [RK, RQ, Rkr, Rqr])
                            pt, rpt = PT[kt % 3], RPT[kt % 3]
                            P.op("act", lambda e, pb=pb, pt=pt: e.activation(out=pt, in_=C.psum[pb], func=AF.Exp),
                                 reads=[C.psr[pb]], writes=[rpt])
                            P.op("pe", lambda e, kt=kt, pt=pt, nkt=nkt: e.matmul(C.psum[5], Vtm[:, kt, :], pt, start=(kt == 0), stop=(kt == nkt - 1)),
                                 reads=[RV, rpt], writes=[C.psr[5]], sig=True)
                            if kt == 0:
                                P.op("dve", lambda e, pt=pt: e.tensor_copy(Pacc, pt), reads=[rpt], writes=[Racc])
                            else:
                                P.op("dve", lambda e, pt=pt: e.tensor_tensor(Pacc, Pacc, pt, ALU.add), reads=[rpt, Racc], writes=[Racc])
                        mm(C.psum[6], C.psr[6], [(ones_f, Pacc)], [Racc, R_const])
                        P.op("dve", lambda e: e.reciprocal(rcp, C.psum[6]), reads=[C.psr[6]], writes=[Rrcp])
                        ob, rob = osb[ocnt % 2], Ros[ocnt % 2]
                        ocnt += 1
                        P.op("dve", lambda e, ob=ob: e.tensor_tensor(ob, C.psum[5], rcp, ALU.mult), reads=[C.psr[5], Rrcp], writes=[rob])
                        P.dma("pool", oscr[s][hd * 128:(hd + 1) * 128, qsl], ob, reads=[rob], writes=[C.scr_res])
            attn()
            P.barrier()
            ar.reset(m)
        for s in range(nseq):
            seq_body(s)
        mixer_post(l, "mla_w_out", [2, D, D], i, oscr)
    C.mla_layer = mla_layer


    def fnet_layer(l):
        i = l // 3
        cs4d = din("fn_cs4", [128, 512])
        bT = din("fnet_bT", [128, DC])
        hbuf = [dscr("fn_h%d" % s_, [D, S_list[s_]], BF16) for s_ in range(nseq)]
        fbuf = [dscr("fn_f%d" % s_, [D, S_list[s_]], BF16) for s_ in range(nseq)]
        gd = [dscr("fn_g%d" % s_, [2, S_list[s_] // 128, 128, D], BF16) for s_ in range(nseq)]

        def pass1():
            m = ar.mark()
            xt = ar.alloc([DC, T], F32)
            hT = ar.alloc([DC, T], BF16)
            sq = ar.alloc([DC, T], BF16)
            rstd = ar.alloc([T], F32)
            tmp2 = [ar.alloc([T], F32) for _ in range(2)]
            Rx, Rh, Rsq, Rr = (Res(n) for n in "x h sq r".split())
            Rt = [Res("t0"), Res("t1")]
            for s_ in range(nseq):
                for t in range(S_list[s_] // T):
                    P.dma("sp", xt, xview(C.xsrc[s_], t), reads=[C.xres[s_][t]], writes=[Rx])
                    norm_mod(xt, Rx, l, 1, s_, hT, Rh, sq, Rsq, rstd, Rr, tmp2, Rt, 0)
                    P.dma("pool", xview(hbuf[s_], t), hT, reads=[Rh], writes=[C.scr_res])
            P.barrier()
            ar.reset(m)
        pass1()

        def pass23(s_):
            S = S_list[s_]
            N2 = S // 128
            SB = min(8, N2)
            twd = din("fn_tw%d" % N2, [128, N2 * 256])
            wd_ = din("fn_w%d" % N2, [2 * N2, N2])
            m = ar.mark()
            CS4 = ar.alloc([512], BF16)
            TW = ar.alloc([N2, 2, 128], BF16)
            WCS = ar.alloc([N2], BF16)
            Rc = Res("fc")
            P.dma("pool", CS4, cs4d, writes=[Rc])
            P.dma("pool", TW, twd.rearrange("p (a b c) -> p a b c", a=N2, b=2), writes=[Rc])
            P.dma("pool", WCS[0:2 * N2, :], wd_, writes=[Rc])
            m2 = ar.mark()

            def stage_ab():
                hg2 = [ar.alloc([S], BF16) for _ in range(2)]
                Rhg = [Res("hg0"), Res("hg1")]
                zb = [ar.alloc([512], BF16) for _ in range(2)]
                Rz = [Res("z0"), Res("z1")]
                gs2 = [ar.alloc([SB, 2, 128], BF16) for _ in range(2)]
                Rgs = [Res("gs0"), Res("gs1")]
                cnt = 0
                for g in range(8):
                    hg, rhg = hg2[g % 2], Rhg[g % 2]
                    P.dma("sp", hg, hbuf[s_][g * 128:(g + 1) * 128, :], reads=[C.scr_res], writes=[rhg])
                    hv = hg.rearrange("p (s1 n2) -> p n2 s1", n2=N2)
                    for sb in range(N2 // SB):
                        gs, rgs = gs2[cnt % 2], Rgs[cnt % 2]
                        cnt += 1
                        for j in range(SB):
                            s2 = sb * SB + j
                            pa, pb = j % 2, 2 + (j % 2)
                            z, rz = zb[j % 2], Rz[j % 2]
                            mm(C.psum[pa], C.psr[pa], [(hv[:, s2, :], CS4)], [rhg, Rc])
                            P.op("act", lambda e, z=z, pa=pa: e.activation(out=z, in_=C.psum[pa], func=AF.Copy),
                                 reads=[C.psr[pa]], writes=[rz])
                            mm(C.psum[pb][:, 0:256], C.psr[pb], [(TW[:, s2, 0, :], z[:, 0:256]), (TW[:, s2, 1, :], z[:, 256:512])], [rz, Rc])
                            P.op("dve", lambda e, gs=gs, j=j, pb=pb: e.tensor_copy(
                                gs[:, j, :, :], C.psum[pb][:, 0:256].rearrange("p (r m) -> p r m", r=2)),
                                reads=[C.psr[pb]], writes=[rgs])
                        for ri in range(2):
                            dst = gd[s_][ri, sb * SB:(sb + 1) * SB, :, g * 128:(g + 1) * 128].rearrange("n k m -> k n m")
                            P.dma("pool", dst, gs[:, :, ri, :], reads=[rgs], writes=[C.scr_res])
                P.barrier()
            stage_ab()
            ar.reset(m2)

            def stage_c():
                fT = ar.alloc([DC, S], BF16)
                Rf = Res("fT")
                KB = 8
                g2 = [ar.alloc([KB, D], BF16) for _ in range(2)]
                Rg2 = [Res("g20"), Res("g21")]
                for kb in range(128 // KB):
                    G2, rg2 = g2[kb % 2], Rg2[kb % 2]
                    for ri in range(2):
                        P.dma("sp", G2[ri * N2:(ri + 1) * N2, :, :], gd[s_][ri, :, kb * KB:(kb + 1) * KB, :],
                              reads=[C.scr_res], writes=[rg2])
                    for g in range(8):
                        pb = g % 4
                        for j in range(KB):
                            mm(C.psum[pb][:, j * N2:(j + 1) * N2], C.psr[pb],
                               [(G2[0:2 * N2, j, g * 128:(g + 1) * 128], WCS[0:2 * N2, :])], [rg2, Rc])
                        dstv = fT[:, g, :].rearrange("p (k2 k1) -> p k1 k2", k1=128)[:, kb * KB:(kb + 1) * KB, :]
                        srcv = C.psum[pb][:, 0:KB * N2].rearrange("p (a b) -> p a b", a=KB)
                        if g % 2 == 0:
                            P.op("act", lambda e, dstv=dstv, srcv=srcv: e.activation(out=dstv, in_=srcv, func=AF.Copy),
                                 reads=[C.psr[pb]], writes=[Rf])
                        else:
                            P.op("dve", lambda e, dstv=dstv, srcv=srcv: e.tensor_copy(dstv, srcv),
                                 reads=[C.psr[pb]], writes=[Rf])
                for c in range(DC):
                    P.dma("pool", fbuf[s_][c * 128:(c + 1) * 128, :], fT[:, c, :], reads=[Rf], writes=[C.scr_res])
                P.barrier()
            stage_c()
            ar.reset(m)
        for s_ in range(nseq):
            pass23(s_)
        mb = ar.mark()
        bsb = ar.alloc([DC], F32)
        P.dma("sp", bsb, bT, writes=[R_const])
        mixer_post(l, "fnet_w_out", [1, D, D], i, fbuf, bias_col=lambda o: bsb[:, o:o + 1])
        ar.reset(mb)
    C.fnet_layer = fnet_layer


    def gdn_layer(l):
        i = l // 3
        w_in = din("gdn_w_in", [1, D, 4128])
        convT = din("gdn_convT", [128, 24 * 5])
        gcon = din("gdn_consts", [64, 64 * 7 + 256])
        gab = din("gdn_ab", [1, 32])
        onT = din("gdn_onT", [128, 1])
        pj = [dscr("gd_pj%d" % s_, [4096, S_list[s_]], BF16) for s_ in range(nseq)]
        ab = [dscr("gd_ab%d" % s_, [S_list[s_], 32], F32) for s_ in range(nseq)]
        ofw = [dscr("gd_of%d" % s_, [8, S_list[s_], 128], F32) for s_ in range(nseq)]
        oscr = [dscr("gd_o%d" % s_, [D, S_list[s_]], BF16) for s_ in range(nseq)]
        bank = [0]

        def nb():
            bank[0] = (bank[0] + 1) % 8
            return bank[0]

        def pass1():
            m = ar.mark()
            Win = ar.alloc([DC, 4128], BF16)
            xt = ar.alloc([DC, T], F32)
            hT = ar.alloc([DC, T], BF16)
            sq = ar.alloc([DC, T], BF16)
            rstd = ar.alloc([T], F32)
            tmp2 = [ar.alloc([T], F32) for _ in range(2)]
            pb4 = [ar.alloc([4, T], BF16) for _ in range(2)]
            abt = ar.alloc([4, 32], F32)
            Rw, Rx, Rh, Rsq, Rr, Rab = (Res(n) for n in "w x h sq r ab".split())
            Rt = [Res("t0"), Res("t1")]
            Rp4 = [Res("p0"), Res("p1")]
            wv = w_in[i].rearrange("(k p) n -> p k n", p=128)
            for k in range(DC):
                P.dma("pool", Win[:, k, :], wv[:, k, :], writes=[Rw])
            cnt = 0
            for s_ in range(nseq):
                for t in range(S_list[s_] // T):
                    tsl = slice(t * T, (t + 1) * T)
                    P.dma("sp", xt, xview(C.xsrc[s_], t), reads=[C.xres[s_][t]], writes=[Rx])
                    norm_mod(xt, Rx, l, 1, s_, hT, Rh, sq, Rsq, rstd, Rr, tmp2, Rt, 0)
                    for c4 in range(8):
                        buf, rb = pb4[cnt % 2], Rp4[cnt % 2]
                        cnt += 1
                        for cc in range(4):
                            ch = c4 * 4 + cc
                            pb = 1 + (ch % 4)
                            mm(C.psum[pb], C.psr[pb], [(Win[:, k, ch * 128:(ch + 1) * 128], hT[:, k, :]) for k in range(DC)], [Rw, Rh])
                            if cc % 2 == 0:
                                P.op("act", lambda e, buf=buf, cc=cc, pb=pb: e.activation(out=buf[:, cc, :], in_=C.psum[pb], func=AF.Copy),
                                     reads=[C.psr[pb]], writes=[rb])
                            else:
                                P.op("dve", lambda e, buf=buf, cc=cc, pb=pb: e.tensor_copy(buf[:, cc, :], C.psum[pb]),
                                     reads=[C.psr[pb]], writes=[rb])
                        P.dma("pool", pj[s_][c4 * 512:(c4 + 1) * 512, tsl].rearrange("(c p) t -> p c t", p=128), buf, reads=[rb], writes=[C.scr_res])
                    for tt in range(4):
                        mm(C.psum[6][:, tt * 32:(tt + 1) * 32], C.psr[6],
                           [(hT[:, k, tt * 128:(tt + 1) * 128], Win[:, k, 4096:4128]) for k in range(DC)], [Rw, Rh])
                    P.op("act", lambda e: e.activation(out=abt, in_=C.psum[6][:, 0:128].rearrange("p (a b) -> p a b", a=4), func=AF.Copy),
                         reads=[C.psr[6]], writes=[Rab])
                    P.dma("pool", ab[s_][tsl, :].rearrange("(a p) c -> p a c", p=128), abt, reads=[Rab], writes=[C.scr_res])
            P.barrier()
            ar.reset(m)
        pass1()

        def core(s_):
            S = S_list[s_]
            N = S // 64
            m = ar.mark()
            GC = ar.alloc([64 * 7 + 256], F32)
            I64b = ar.alloc([64], BF16)
            I128b = ar.alloc([128], BF16)
            cvw = ar.alloc([24, 5], F32)
            onw = ar.alloc([1], F32)
            one_c = ar.alloc([1], F32)
            abc = ar.alloc([32], F32)
            Rc = Res("gc")
            P.dma("sp", GC[0:64, :], gcon, writes=[Rc])
            P.dma("sp", cvw, convT.rearrange("p (a b) -> p a b", a=24), writes=[Rc])
            P.dma("sp", onw, onT, writes=[Rc])
            P.dma("sp", abc[0:64, :], gab.partition_broadcast(64), writes=[Rc])
            P.op("pool", lambda e: e.memset(one_c, 1.0), writes=[Rc])
            TRI = [GC[0:64, 0:64], GC[0:64, 64:128]]
            NEGI = [GC[0:64, 128:192], GC[0:64, 256:320]]
            NEGS = [GC[0:64, 192:256], GC[0:64, 320:384]]
            I64f = GC[0:64, 384:448]
            SEL = [GC[0:64, 448:576], GC[0:64, 576:704]]
            P.op("dve", lambda e: e.tensor_copy(I64b[0:64, :], I64f), reads=[Rc], writes=[Rc])
            P.op("pool", lambda e: e.memset(I128b, 0.0), writes=[Rc])
            P.op("pool", lambda e: e.affine_select(out=I128b, in_=I128b, pattern=[[-1, 128]], compare_op=ALU.not_equal,
                                                   fill=1.0, base=0, channel_multiplier=1), reads=[Rc], writes=[Rc])
            P.op("act", lambda e: e.activation(out=abc[0:64, 0:16], in_=abc[0:64, 0:16], func=AF.Exp), reads=[Rc], writes=[Rc])
            P.op("dve", lambda e: e.tensor_scalar(abc[0:64, 0:16], abc[0:64, 0:16], -1.0, None, ALU.mult), reads=[Rc], writes=[Rc])
            GN = ("beta", "gc", "ngc", "gcb", "egc", "bege", "ekd")
            gsc = dscr("gd_gs%d" % s_, [len(GN), 16, 64, N], F32)
            eglsc = dscr("gd_egl%d" % s_, [16, 128, N], F32)
            Rg = Res("gates")

            def gates():
                mg = ar.mark()
                G = {nm: ar.alloc([16, N], F32) for nm in ("g",) + GN}
                EGL = ar.alloc([16, N], F32)
                abs_ = ar.alloc([N, 32], F32)
                P.dma("sp", abs_[0:64], ab[s_].rearrange("(n t) c -> t n c", t=64), reads=[C.scr_res], writes=[Rg])
                a_ = abs_[0:64, :, 0:16].rearrange("p n c -> p c n")
                b_ = abs_[0:64, :, 16:32].rearrange("p n c -> p c n")
                g, beta, gc, ngc, gcb, egc, bege, ekd = (G[k][0:64] for k in ("g", "beta", "gc", "ngc", "gcb", "egc", "bege", "ekd"))
                P.op("dve", lambda e: e.tensor_tensor(g, a_, bcast_free(abc[0:64, 16:32], N, 2), ALU.add), reads=[Rg, Rc], writes=[Rg])
                P.op("act", lambda e: e.activation(out=g, in_=g, func=AF.Exp), reads=[Rg], writes=[Rg])
                P.op("act", lambda e: e.activation(out=g, in_=g, func=AF.Ln, bias=one_c[0:64, :]), reads=[Rg, Rc], writes=[Rg])
                P.op("dve", lambda e: e.tensor_tensor(g, g, bcast_free(abc[0:64, 0:16], N, 2), ALU.mult), reads=[Rg, Rc], writes=[Rg])
                P.op("act", lambda e: e.activation(out=beta, in_=b_, func=AF.Sigmoid), reads=[Rg], writes=[Rg])
                P.op("act", lambda e: e.activation(out=gcb, in_=beta, func=AF.Ln), reads=[Rg], writes=[Rg])
                NBK = 32
                for d_ in range(2):
                    for n0 in range(0, N, NBK):
                        n1 = min(N, n0 + NBK)
                        w = (n1 - n0) * 8
                        pb = nb()
                        mm(C.psum[pb][0:64, 0:w], C.psr[pb], [(TRI[d_], g[:, d_ * 8:(d_ + 1) * 8, n0:n1])], [Rg, Rc])
                        P.op("act", lambda e, pb=pb, w=w, n0=n0, n1=n1, d_=d_: e.activation(
                            out=gc[:, d_ * 8:(d_ + 1) * 8, n0:n1], in_=C.psum[pb][0:64, 0:w].rearrange("p (a b) -> p a b", a=8), func=AF.Copy),
                            reads=[C.psr[pb]], writes=[Rg])
                        pb2 = nb()
                        mm(C.psum[pb2][:, 0:w], C.psr[pb2], [(SEL[d_], gc[:, d_ * 8:(d_ + 1) * 8, n0:n1])], [Rg, Rc])
                        P.op("act", lambda e, pb2=pb2, w=w, n0=n0, n1=n1, d_=d_: e.activation(
                            out=EGL[:, d_ * 8:(d_ + 1) * 8, n0:n1], in_=C.psum[pb2][:, 0:w].rearrange("p (a b) -> p a b", a=8), func=AF.Exp),
                            reads=[C.psr[pb2]], writes=[Rg])
                        P.op("dve", lambda e, pb2=pb2, w=w, n0=n0, n1=n1, d_=d_: e.tensor_tensor(
                            ekd[:, d_ * 8:(d_ + 1) * 8, n0:n1], C.psum[pb2][0:64, 0:w].rearrange("p (a b) -> p a b", a=8),
                            gc[:, d_ * 8:(d_ + 1) * 8, n0:n1], ALU.subtract),
                            reads=[C.psr[pb2], Rg], writes=[Rg])
                P.op("act", lambda e: e.activation(out=ekd, in_=ekd, func=AF.Exp), reads=[Rg], writes=[Rg])
                P.op("act", lambda e: e.activation(out=egc, in_=gc, func=AF.Exp), reads=[Rg], writes=[Rg])
                P.op("dve", lambda e: e.tensor_tensor(bege, beta, egc, ALU.mult), reads=[Rg], writes=[Rg])
                P.op("dve", lambda e: e.tensor_tensor(gcb, gcb, gc, ALU.add), reads=[Rg], writes=[Rg])
                P.op("dve", lambda e: e.tensor_scalar(ngc, gc, -1.0, None, ALU.mult), reads=[Rg], writes=[Rg])
                for qi, nm in enumerate(GN):
                    P.dma("pool", gsc[qi].rearrange("c t n -> t c n"), G[nm][0:64], reads=[Rg], writes=[C.scr_res])
                P.dma("pool", eglsc.rearrange("c t n -> t c n"), EGL, reads=[Rg], writes=[C.scr_res])
                P.barrier()
                ar.reset(mg)
            if GDN_STAGE >= 2:
                gates()

            def head(hd):
                mh = ar.mark()
                QT = ar.alloc([S], BF16)
                KT = ar.alloc([S], BF16)
                Ktm = ar.alloc([N, 128], BF16)
                Vtm = ar.alloc([N, 128], BF16)
                RQ, RK, RKt, RVt = Res("Q"), Res("K"), Res("Kt"), Res("Vt")
                Gh = {nm: ar.alloc([2, N], F32) for nm in GN}
                EGLh = ar.alloc([2, N], F32)
                Rgh = Res("gh")
                for qi, nm in enumerate(GN):
                    for d2 in range(2):
                        P.dma("sp", Gh[nm][0:64, d2, :], gsc[qi, d2 * 8 + hd], reads=[C.scr_res], writes=[Rgh])
                for d2 in range(2):
                    P.dma("sp", EGLh[:, d2, :], eglsc[d2 * 8 + hd], reads=[C.scr_res], writes=[Rgh])

                def prep():
                    mp = ar.mark()
                    raw = ar.alloc([3, S + 4], BF16)
                    VT = ar.alloc([S], BF16)
                    acc = [ar.alloc([T], F32) for _ in range(2)]
                    sl = [ar.alloc([T], F32) for _ in range(2)]
                    sqb = ar.alloc([T], BF16)
                    rs = ar.alloc([T], F32)
                    Rraw, RVT, Rsqb, Rrs = Res("raw"), Res("VT"), Res("sqb"), Res("rs")
                    Racc = [Res("a0"), Res("a1")]
                    Rsl = [Res("s0"), Res("s1")]
                    P.op("pool", lambda e: e.memset(raw[:, :, 0:2], 0.0), writes=[Rraw])
                    P.op("pool", lambda e: e.memset(raw[:, :, S + 2:S + 4], 0.0), writes=[Rraw])
                    for j in range(3):
                        P.dma("sp", raw[:, j, 2:S + 2], pj[s_][j * 1024 + hd * 128:j * 1024 + (hd + 1) * 128, :], reads=[C.scr_res], writes=[Rraw])
                    k_ = 0
                    for j in range(3):
                        for t in range(S // T):
                            a, ra = acc[k_ % 2], Racc[k_ % 2]
                            so, rso = sl[k_ % 2], Rsl[k_ % 2]
                            k_ += 1
                            tsl = slice(t * T, (t + 1) * T)
                            ch = j * 8 + hd
                            P.op("dve", lambda e, a=a, j=j, t=t, ch=ch: e.tensor_scalar(a, raw[:, j, t * T:t * T + T], cvw[:, ch, 0:1], None, ALU.mult),
                                 reads=[Rraw, Rc], writes=[ra])
                            for tap in range(1, 5):
                                P.op("dve", lambda e, a=a, j=j, t=t, ch=ch, tap=tap: e.scalar_tensor_tensor(
                                    a, raw[:, j, t * T + tap:t * T + tap + T], cvw[:, ch, tap:tap + 1], a, ALU.mult, ALU.add),
                                    reads=[Rraw, Rc, ra], writes=[ra])
                            if j == 2:
                                P.op("act", lambda e, a=a, tsl=tsl: e.activation(out=VT[:, tsl], in_=a, func=AF.Silu), reads=[ra], writes=[RVT])
                                continue
                            P.op("act", lambda e, a=a, so=so: e.activation(out=so, in_=a, func=AF.Silu), reads=[ra], writes=[rso])
                            P.op("pool", lambda e, so=so: e.tensor_tensor(sqb, so, so, ALU.mult), reads=[rso], writes=[Rsqb])
                            pb = nb()
                            mm(C.psum[pb], C.psr[pb], [(ones_bf, sqb)], [Rsqb, R_const])
                            rsqrt_ps(rs, Rrs, C.psum[pb], C.psr[pb], scale=float(D))
                            dstT, rdst, scl = (QT, RQ, 128.0 ** -0.5) if j == 0 else (KT, RK, 1.0)
                            P.op("dve", lambda e, so=so, dstT=dstT, tsl=tsl, scl=scl: e.scalar_tensor_tensor(
                                dstT[:, tsl], so, scl, rs, ALU.mult, ALU.mult), reads=[rso, Rrs], writes=[rdst])
                    for (srcT, rsrc, dst, rdst) in ((KT, RK, Ktm, RKt), (VT, RVT, Vtm, RVt)):
                        for n4 in range(N // 4):
                            pb = nb()
                            for q in range(4):
                                n = n4 * 4 + q
                                mm(C.psum[pb][0:64, q * 128:(q + 1) * 128], C.psr[pb], [(srcT[:, n * 64:(n + 1) * 64], I128b)], [rsrc, Rc])
                            P.op("act", lambda e, pb=pb, dst=dst, n4=n4: e.activation(
                                out=dst[0:64, n4 * 4:(n4 + 1) * 4, :], in_=C.psum[pb][0:64, :].rearrange("p (a b) -> p a b", a=4), func=AF.Copy),
                                reads=[C.psr[pb]], writes=[rdst])
                    P.barrier()
                    ar.reset(mp)
                if GDN_STAGE >= 3:
                    prep()

                def direction(d_):
                    md = ar.mark()
                    c = d_ * 8 + hd
                    NBc = 8
                    Sf = ar.alloc([128], F32)
                    Sb = ar.alloc([128], BF16)
                    RS, RSb = Res("S"), Res("Sb")
                    P.op("pool", lambda e: e.memset(Sf, 0.0), writes=[RS])
                    P.op("pool", lambda e: e.memset(Sb, 0.0), writes=[RSb])
                    bV = ar.alloc([NBc, 128], F32)
                    bgK = ar.alloc([NBc, 128], F32)
                    Kd = ar.alloc([NBc, 128], BF16)
                    ngB = ar.alloc([NBc, 64], F32)
                    RngB = Res("ngB")
                    Dbs = ar.alloc([NBc * 64], F32)
                    Dis = ar.alloc([NBc * 64], F32)
                    Nbf = [ar.alloc([NBc, 64], F32) for _ in range(2)]
                    Mb = [ar.alloc([NBc, 64], F32) for _ in range(2)]
                    Pb = [ar.alloc([NBc, 64], F32) for _ in range(2)]
                    QKD = ar.alloc([NBc, 64], BF16)
                    QKDT = ar.alloc([NBc, 64], BF16)
                    U = ar.alloc([NBc, 128], F32)
                    WT = ar.alloc([NBc, 64], F32)
                    Vn = [ar.alloc([128], BF16) for _ in range(2)]
                    o1 = [ar.alloc([128], F32) for _ in range(2)]
                    Ob = ar.alloc([NBc, 128], F32)
                    Of = ar.alloc([NBc, 128], F32)
                    osq = ar.alloc([NBc, 128], F32)
                    ors = ar.alloc([NBc], F32)
                    onb = ar.alloc([NBc, 128], BF16)
                    gt = ar.alloc([NBc * 64], BF16)
                    sgt = ar.alloc([NBc * 64], BF16)
                    oTb = ar.alloc([NBc * 64], BF16)
                    names = "bV bgK Kd Db Di QKD QKDT U WT Ob Of osq ors onb gt sgt oTb".split()
                    Rr = {n_: Res(n_) for n_ in names}
                    RM = [Res("M0"), Res("M1")]
                    RN = [Res("N0"), Res("N1")]
                    RP = [Res("P0"), Res("P1")]
                    RVn = [Res("Vn0"), Res("Vn1")]
                    Ro1 = [Res("o10"), Res("o11")]
                    NEGSt = bcast_free(NEGS[d_], NBc, 1)
                    NEGIt = bcast_free(NEGI[d_], NBc, 1)
                    blocks = list(range(N // NBc))
                    if d_ == 1:
                        blocks = blocks[::-1]
                    vcnt = 0
                    for bk in blocks:
                        n0 = bk * NBc
                        nsl = slice(n0, n0 + NBc)
                        for (dst, src, gq, rn, rs_) in ((bV, Vtm, "beta", "bV", RVt), (bgK, Ktm, "bege", "bgK", RKt), (Kd, Ktm, "ekd", "Kd", RKt)):
                            P.op("dve", lambda e, dst=dst, src=src, gq=gq, nsl=nsl: e.tensor_tensor(
                                dst[0:64], src[0:64, nsl, :], bcast_free(Gh[gq][0:64, d_, nsl], 128, 2), ALU.mult),
                                reads=[rs_, Rgh], writes=[Rr[rn]])
                        P.op("dve", lambda e, nsl=nsl: e.tensor_copy(ngB[0:64], bcast_free(Gh["ngc"][0:64, d_, nsl], 64, 2)),
                             reads=[Rgh], writes=[RngB])
                        pkk, pqk, pdb, pdi = nb(), nb(), nb(), nb()
                        for j in range(NBc):
                            n = n0 + j
                            csl = slice(n * 64, (n + 1) * 64)
                            mm(C.psum[pkk][0:64, j * 64:(j + 1) * 64], C.psr[pkk], [(KT[:, csl], KT[:, csl])], [RK])
                            mm(C.psum[pqk][0:64, j * 64:(j + 1) * 64], C.psr[pqk], [(QT[:, csl], KT[:, csl])], [RK, RQ])
                        for (pd, NEGt, gq) in ((pdb, NEGSt, "gcb"), (pdi, NEGIt, "gc")):
                            P.op("pe", lambda e, pd=pd, NEGt=NEGt: e.matmul(C.psum[pd][0:64, :].rearrange("p (a b) -> p a b", a=NBc), I64f, NEGt, start=True, stop=False),
                                 reads=[Rc], writes=[C.psr[pd]], sig=False)
                            for j in range(NBc):
                                n = n0 + j
                                last = (j == NBc - 1)
                                P.op("pe", lambda e, pd=pd, j=j, n=n, gq=gq: e.matmul(
                                    C.psum[pd][0:64, j * 64:(j + 1) * 64], I64f, bcast_free(Gh[gq][0:64, d_, n], 64, 1), start=False, stop=False),
                                    reads=[Rc, Rgh], writes=[C.psr[pd]], sig=False)
                                P.op("pe", lambda e, pd=pd, j=j, n=n, last=last: e.matmul(
                                    C.psum[pd][0:64, j * 64:(j + 1) * 64], ngB[0:64, j, :], I64f, start=False, stop=last),
                                    reads=[Rc, Rgh, RngB], writes=[C.psr[pd]], sig=last)
                        P.op("act", lambda e, pdb=pdb: e.activation(out=Dbs[0:64], in_=C.psum[pdb][0:64, :], func=AF.Exp), reads=[C.psr[pdb]], writes=[Rr["Db"]])
                        P.op("act", lambda e, pdi=pdi: e.activation(out=Dis[0:64], in_=C.psum[pdi][0:64, :], func=AF.Exp), reads=[C.psr[pdi]], writes=[Rr["Di"]])
                        P.op("dve", lambda e, pkk=pkk: e.tensor_tensor(Mb[0][0:64].rearrange("p a b -> p (a b)"), C.psum[pkk][0:64, :], Dbs[0:64], ALU.mult),
                             reads=[C.psr[pkk], Rr["Db"]], writes=[RM[0]])
                        P.op("dve", lambda e, pqk=pqk: e.tensor_tensor(QKD[0:64].rearrange("p a b -> p (a b)"), C.psum[pqk][0:64, :], Dis[0:64], ALU.mult),
                             reads=[C.psr[pqk], Rr["Di"]], writes=[Rr["QKD"]])
                        if GDN_SUB < 2:
                            continue
                        pn, pq = nb(), nb()
                        for j in range(NBc):
                            mm(C.psum[pn][0:64, j * 64:(j + 1) * 64], C.psr[pn], [(Mb[0][0:64, j, :], I64f)], [RM[0], Rc])
                            mm(C.psum[pq][0:64, j * 64:(j + 1) * 64], C.psr[pq], [(QKD[0:64, j, :], I64b[0:64, :])], [Rr["QKD"], Rc])
                        P.op("dve", lambda e, pn=pn: e.tensor_copy(Nbf[0][0:64].rearrange("p a b -> p (a b)"), C.psum[pn][0:64, :]),
                             reads=[C.psr[pn]], writes=[RN[0]])
                        P.op("dve", lambda e, pn=pn: e.scalar_tensor_tensor(Pb[0][0:64], C.psum[pn][0:64, :].rearrange("p (a b) -> p a b", a=NBc), -1.0,
                                                                              bcast_free(I64f, NBc, 1), ALU.mult, ALU.add),
                             reads=[C.psr[pn], Rc], writes=[RP[0]])
                        P.op("dve", lambda e, pq=pq: e.tensor_copy(QKDT[0:64].rearrange("p a b -> p (a b)"), C.psum[pq][0:64, :]),
                             reads=[C.psr[pq]], writes=[Rr["QKDT"]])
                        cm, cn, cp = 0, 0, 0
                        if GDN_SUB < 1.5:
                            continue
                        for lvl in range(GDN_NLV):
                            pm = nb()
                            for j in range(NBc):
                                if GDN_X in (3, 7):
                                    break
                                if GDN_X in (4, 8):
                                    mm(C.psum[pm][0:64, j * 64:(j + 1) * 64], C.psr[pm], [(Nbf[cn][0:64, j, :], I64b[0:64, :])], [RN[cn], RM[cm], Rc])
                                    continue
                                if GDN_X == 5:
                                    mm(C.psum[pm][0:64, j * 64:(j + 1) * 64], C.psr[pm], [(Mb[cm][0:64, j, :], Mb[cm][0:64, j, :])], [RN[cn], RM[cm], Rc])
                                    continue
                                if GDN_X == 6:
                                    mm(C.psum[pm][0:64, j * 64:(j + 1) * 64], C.psr[pm], [(I64b[0:64, :], Mb[cm][0:64, j, :])], [RN[cn], RM[cm], Rc])
                                    continue
                                mm(C.psum[pm][0:64, j * 64:(j + 1) * 64], C.psr[pm], [(Nbf[cn][0:64, j, :], Mb[cm][0:64, j, :])], [RN[cn], RM[cm]])
                            if lvl < 4 and GDN_X not in (2, 4, 5, 6, 7, 8):
                                pn2 = nb()
                                for j in range(NBc):
                                    mm(C.psum[pn2][0:64, j * 64:(j + 1) * 64], C.psr[pn2], [(Mb[cm][0:64, j, :], Nbf[cn][0:64, j, :])], [RN[cn], RM[cm]])
                            nm_, nn_ = 1 - cm, 1 - cn
                            if GDN_X not in (3, 8):
                                P.op("dve", lambda e, pm=pm, nm_=nm_: e.tensor_copy(Mb[nm_][0:64].rearrange("p a b -> p (a b)"), C.psum[pm][0:64, :]),
                                     reads=[C.psr[pm]], writes=[RM[nm_]])
                            if lvl < 4 and GDN_X not in (2, 4, 5, 6, 7, 8):
                                P.op("dve", lambda e, pn2=pn2, nn_=nn_: e.tensor_copy(Nbf[nn_][0:64].rearrange("p a b -> p (a b)"), C.psum[pn2][0:64, :]),
                                     reads=[C.psr[pn2]], writes=[RN[nn_]])
                                cn = nn_
                            cm = nm_
                            if GDN_X >= 1:
                                continue
                            pp = nb()
                            for j in range(NBc):
                                mm(C.psum[pp][0:64, j * 64:(j + 1) * 64], C.psr[pp],
                                   [(Mb[cm][0:64, j, :], Pb[cp][0:64, j, :]), (I64f, Pb[cp][0:64, j, :])], [RM[cm], RP[cp], Rc])
                            np_ = 1 - cp
                            P.op("dve", lambda e, pp=pp, np_=np_: e.tensor_copy(Pb[np_][0:64].rearrange("p a b -> p (a b)"), C.psum[pp][0:64, :]),
                                 reads=[C.psr[pp]], writes=[RP[np_]])
                            cp = np_
                        if GDN_SUB < 3:
                            continue
                        Pf, RPf = Pb[cp], RP[cp]
                        pu = [nb(), nb()]
                        pw = nb()
                        for j in range(NBc):
                            mm(C.psum[pu[j // 4]][0:64, (j % 4) * 128:(j % 4 + 1) * 128], C.psr[pu[j // 4]], [(Pf[0:64, j, :], bV[0:64, j, :])], [RPf, Rr["bV"]])
                            mm(C.psum[pw][:, j * 64:(j + 1) * 64], C.psr[pw], [(bgK[0:64, j, :], Pf[0:64, j, :])], [RPf, Rr["bgK"]])
                        for h2 in range(2):
                            P.op("act", lambda e, h2=h2, pu=pu: e.activation(out=U[0:64, h2 * 4:(h2 + 1) * 4, :], in_=C.psum[pu[h2]][0:64, :].rearrange("p (a b) -> p a b", a=4), func=AF.Copy),
                                 reads=[C.psr[pu[h2]]], writes=[Rr["U"]])
                        P.op("dve", lambda e, pw=pw: e.tensor_copy(WT.rearrange("p a b -> p (a b)"), C.psum[pw]), reads=[C.psr[pw]], writes=[Rr["WT"]])
                        if d_ == 1:
                            P.dma("sp", Of[0:64], ofw[s_][hd, n0 * 64:(n0 + NBc) * 64, :].rearrange("(n t) e -> t n e", t=64), reads=[C.scr_res], writes=[Rr["Of"]])
                            P.dma("sp", gt, pj[s_][3072 + hd * 128:3072 + (hd + 1) * 128, n0 * 64:(n0 + NBc) * 64], reads=[C.scr_res], writes=[Rr["gt"]])
                        if GDN_SUB < 4:
                            continue
                        js = list(range(NBc))
                        if d_ == 1:
                            js = js[::-1]
                        for j in js:
                            n = n0 + j
                            csl = slice(n * 64, (n + 1) * 64)
                            pws, pqs, po2, psn = nb(), nb(), nb(), nb()
                            vn, rvn = Vn[vcnt % 2], RVn[vcnt % 2]
                            ot, rot = o1[vcnt % 2], Ro1[vcnt % 2]
                            vcnt += 1
                            mm(C.psum[pws][0:64, 0:128], C.psr[pws], [(WT[:, j, :], Sf)], [Rr["WT"], RS])
                            mm(C.psum[pqs][0:64, 0:128], C.psr[pqs], [(QT[:, csl], Sb)], [RQ, RSb])
                            P.op("dve", lambda e, vn=vn, j=j, pws=pws: e.tensor_tensor(vn[0:64], U[0:64, j, :], C.psum[pws][0:64, 0:128], ALU.subtract),
                                 reads=[Rr["U"], C.psr[pws]], writes=[rvn])
                            mm(C.psum[po2][0:64, 0:128], C.psr[po2], [(QKDT[0:64, j, :], vn[0:64])], [Rr["QKDT"], rvn])
                            mm(C.psum[psn][:, 0:128], C.psr[psn], [(Kd[0:64, j, :], vn[0:64])], [Rr["Kd"], rvn])
                            P.op("dve", lambda e, psn=psn, n=n: e.scalar_tensor_tensor(Sb, Sf, EGLh[:, d_, n:n + 1], C.psum[psn][:, 0:128], ALU.mult, ALU.add),
                                 reads=[RS, C.psr[psn], Rgh], writes=[RSb])
                            P.op("dve", lambda e, psn=psn, n=n: e.scalar_tensor_tensor(Sf, Sf, EGLh[:, d_, n:n + 1], C.psum[psn][:, 0:128], ALU.mult, ALU.add),
                                 reads=[RS, C.psr[psn], Rgh], writes=[RS])
                            P.op("act", lambda e, ot=ot, pqs=pqs, n=n: e.activation(out=ot[0:64], in_=C.psum[pqs][0:64, 0:128], func=AF.Copy,
                                                                                  scale=Gh["egc"][0:64, d_, n:n + 1]),
                                 reads=[C.psr[pqs], Rgh], writes=[rot])
                            P.op("dve", lambda e, ot=ot, po2=po2, j=j: e.tensor_tensor(Ob[0:64, j, :], ot[0:64], C.psum[po2][0:64, 0:128], ALU.add),
                                 reads=[rot, C.psr[po2]], writes=[Rr["Ob"]])
                        if d_ == 0:
                            P.dma("pool", ofw[s_][hd, n0 * 64:(n0 + NBc) * 64, :].rearrange("(n t) e -> t n e", t=64), Ob[0:64], reads=[Rr["Ob"]], writes=[C.scr_res])
                        else:
                            P.op("pool", lambda e: e.tensor_tensor(Ob[0:64], Ob[0:64], Of[0:64], ALU.add), reads=[Rr["Ob"], Rr["Of"]], writes=[Rr["Ob"]])
                            P.op("pool", lambda e: e.tensor_tensor(osq[0:64], Ob[0:64], Ob[0:64], ALU.mult), reads=[Rr["Ob"]], writes=[Rr["osq"]])
                            P.op("dve", lambda e: e.tensor_reduce(ors[0:64], osq[0:64], AX.X, ALU.add), reads=[Rr["osq"]], writes=[Rr["ors"]])
                            P.op("act", lambda e: e.activation(out=ors[0:64], in_=ors[0:64], func=AF.Sqrt, bias=eps_col[0:64, :], scale=1.0 / 128.0),
                                 reads=[Rr["ors"], R_const], writes=[Rr["ors"]])
                            P.op("dve", lambda e: e.reciprocal(ors[0:64], ors[0:64]), reads=[Rr["ors"]], writes=[Rr["ors"]])
                            P.op("dve", lambda e: e.tensor_tensor(onb[0:64], Ob[0:64], bcast_free(ors[0:64], 128, 2), ALU.mult),
                                 reads=[Rr["Ob"], Rr["ors"]], writes=[Rr["onb"]])
                            pt_ = nb()
                            for j in range(NBc):
                                mm(C.psum[pt_][:, j * 64:(j + 1) * 64], C.psr[pt_], [(onb[0:64, j, :], I64b[0:64, :])], [Rr["onb"], Rc])
                            P.op("act", lambda e: e.activation(out=sgt, in_=gt, func=AF.Silu), reads=[Rr["gt"]], writes=[Rr["sgt"]])
                            P.op("dve", lambda e, pt_=pt_: e.scalar_tensor_tensor(oTb, C.psum[pt_], onw[:, 0:1], sgt, ALU.mult, ALU.mult),
                                 reads=[C.psr[pt_], Rr["sgt"], Rc], writes=[Rr["oTb"]])
                            P.dma("pool", oscr[s_][hd * 128:(hd + 1) * 128, n0 * 64:(n0 + NBc) * 64], oTb, reads=[Rr["oTb"]], writes=[C.scr_res])
                    P.barrier()
                    ar.reset(md)
                if GDN_STAGE >= 4:
                    direction(0)
                if GDN_STAGE >= 5:
                    direction(1)
                ar.reset(mh)
            for hd in range(8):
                head(hd)
            P.barrier()
            ar.reset(m)
        for s_ in range(nseq):
            core(s_)
        mixer_post(l, "gdn_w_out", [1, D, D], i, oscr)
    C.gdn_layer = gdn_layer

    def mix_layer(l):
        kind = l % 3
        if kind == 0:
            mla_layer(l)
        elif kind == 1:
            C.gdn_layer(l)
        else:
            C.fnet_layer(l)

    for item in plan:
        if item[0] == "ffn":
            ffn_pass(item[1], item[2])
        elif item[0] == "mix":
            mix_layer(item[1])
        else:
            raise ValueError(item)
    P.barrier()
    P.emit()
    return nc, list(C.declared.keys()), layers


def _swap_halves(w):
    h = w.shape[-1] // 2
    return np.concatenate([w[..., h:], w[..., :h]], axis=-1)


def rope_tables(Smax):
    half = 32
    pos = np.arange(Smax, dtype=np.float32)
    inv_freq = (np.float32(10000.0) ** (-np.arange(half, dtype=np.float32) / np.float32(half))).astype(np.float32)
    ang = (pos[:, None] * inv_freq[None, :]).astype(np.float32)
    cos, sin = np.cos(ang).astype(np.float32), np.sin(ang).astype(np.float32)
    cos2 = np.concatenate([cos, cos], axis=1).T
    sin2s = np.concatenate([-sin, sin], axis=1).T
    return np.ascontiguousarray(cos2), np.ascontiguousarray(sin2s)


def host_layout(inputs, core, names, layers, S_list):
    nseq = 2
    xs = ("x_prompt", "x_sample")
    cs = ("c_prompt", "c_sample")
    Smax = max(S_list)
    m = {}
    for nm in names:
        if nm.startswith("xT"):
            m[nm] = np.ascontiguousarray(inputs[xs[int(nm[2:])]][core].T)
        elif nm == "cT":
            c = np.stack([inputs[n][core] for n in cs], axis=0)
            m[nm] = np.ascontiguousarray(c.reshape(nseq, DC, 128).transpose(2, 1, 0).reshape(128, DC * nseq))
        elif nm == "w_ada":
            m[nm] = np.ascontiguousarray(inputs["w_ada"][layers])
        elif nm == "b_adaT":
            m[nm] = np.ascontiguousarray(inputs["b_ada"].reshape(DEPTH, 72, 128).transpose(2, 0, 1).reshape(128, DEPTH * 72))
        elif nm == "npreT":
            m[nm] = np.ascontiguousarray(inputs["norm_pre"].reshape(DEPTH, 3, DC, 128).transpose(3, 0, 1, 2).reshape(128, -1))
        elif nm == "npostT":
            m[nm] = np.ascontiguousarray(inputs["norm_post"].reshape(DEPTH, 3, DC, 128).transpose(3, 0, 1, 2).reshape(128, -1))
        elif nm == "mla_w_down_sw":
            m[nm] = np.ascontiguousarray(_swap_halves(inputs["mla_w_down"][:, :, 640:704]))
        elif nm == "mla_w_uq_sw":
            w = inputs["mla_w_uq"].reshape(2, 384, 8, 192)[:, :, :, 128:192]
            m[nm] = np.ascontiguousarray(_swap_halves(w).reshape(2, 384, 512))
        elif nm == "mla_qnT":
            m[nm] = np.ascontiguousarray(inputs["mla_q_norm"].reshape(2, 3, 128).transpose(2, 0, 1).reshape(128, 6))
        elif nm == "mla_kvnT":
            m[nm] = np.ascontiguousarray(inputs["mla_kv_norm"].reshape(2, 2, 128).transpose(2, 0, 1).reshape(128, 4))
        elif nm == "rope_cos2":
            m[nm] = rope_tables(Smax)[0]
        elif nm == "rope_sin2s":
            m[nm] = rope_tables(Smax)[1]
        elif nm in HOST_EXTRA:
            m[nm] = HOST_EXTRA[nm](inputs, core, S_list)
        else:
            m[nm] = inputs[nm]
    return m


HOST_EXTRA = {}


def _gdn_consts(inputs, core, S_list):
    i = np.arange(64)[:, None]; j = np.arange(64)[None, :]
    NEG = -30000.0
    tri_fw = (i <= j).astype(np.float32)
    tri_bw = (i >= j).astype(np.float32)
    negi_fw = np.where(i >= j, 0.0, NEG); negs_fw = np.where(i > j, 0.0, NEG)
    negi_bw = np.where(i <= j, 0.0, NEG); negs_bw = np.where(i < j, 0.0, NEG)
    eye = np.eye(64)
    sel_fw = np.zeros((64, 128)); sel_fw[63, :] = 1.0
    sel_bw = np.zeros((64, 128)); sel_bw[0, :] = 1.0
    return np.ascontiguousarray(np.concatenate([tri_fw, tri_bw, negi_fw, negs_fw, negi_bw, negs_bw, eye, sel_fw, sel_bw], axis=1).astype(np.float32))


HOST_EXTRA["gdn_consts"] = _gdn_consts
HOST_EXTRA["gdn_convT"] = lambda inputs, core, S_list: np.ascontiguousarray(inputs["gdn_conv"][0].reshape(5, 24, 128).transpose(2, 1, 0).reshape(128, 120))
HOST_EXTRA["gdn_ab"] = lambda inputs, core, S_list: np.ascontiguousarray(np.concatenate([inputs["gdn_a_log"][0].reshape(16), inputs["gdn_dt_bias"][0].reshape(16)])[None, :])
HOST_EXTRA["gdn_onT"] = lambda inputs, core, S_list: np.ascontiguousarray(inputs["gdn_o_norm"][0].reshape(128, 1))


def _fn_cs4(inputs, core, S_list):
    c = np.arange(128)[:, None].astype(np.float64); m_ = np.arange(128)[None, :].astype(np.float64)
    a = 2 * np.pi * c * m_ / 128.0
    Cm, Sm = np.cos(a) / np.sqrt(128.0), np.sin(a) / np.sqrt(128.0)
    return np.ascontiguousarray(np.concatenate([Cm, -Sm, -Sm, -Cm], axis=1).astype(np.float32))


def _fn_tw(N2):
    def f(inputs, core, S_list):
        S = 128 * N2
        s1 = np.arange(128, dtype=np.float64)[:, None, None]
        s2 = np.arange(N2, dtype=np.float64)[None, :, None]
        k1 = np.arange(128, dtype=np.float64)[None, None, :]
        a = 2 * np.pi * np.mod(k1 * (N2 * s1 + s2), S) / S
        tw = np.stack([np.cos(a), np.sin(a)], axis=2) / np.sqrt(128.0)
        return np.ascontiguousarray(tw.reshape(128, N2 * 256).astype(np.float32))
    return f


def _fn_w(N2):
    def f(inputs, core, S_list):
        s2 = np.arange(N2, dtype=np.float64)[:, None]; k2 = np.arange(N2, dtype=np.float64)[None, :]
        a = 2 * np.pi * np.mod(s2 * k2, N2) / N2
        return np.ascontiguousarray((np.concatenate([np.cos(a), np.sin(a)], axis=0) / np.sqrt(N2)).astype(np.float32))
    return f


HOST_EXTRA["fn_cs4"] = _fn_cs4
for _n2 in (4, 8, 16, 32, 64):
    HOST_EXTRA["fn_tw%d" % _n2] = _fn_tw(_n2)
    HOST_EXTRA["fn_w%d" % _n2] = _fn_w(_n2)
HOST_EXTRA["fnet_bT"] = lambda inputs, core, S_list: np.ascontiguousarray(inputs["fnet_b_out"][0].reshape(DC, 128).T)

FULL_PLAN = []
for _l in range(DEPTH):
    FULL_PLAN += [("ffn", _l, 0), ("mix", _l), ("ffn", _l, 2)]


def run(inputs, plan, n_cores=N_CORES):
    S_list = [inputs["x_prompt"].shape[1], inputs["x_sample"].shape[1]]
    nc, names, layers = build(S_list, plan)
    in_maps = [host_layout(inputs, c, names, layers, S_list) for c in range(n_cores)]
    res = run_bass_kernel_spmd(nc, in_maps, core_ids=list(range(n_cores)))
    if DEBUG:
        LAST["res"] = res.results
    outs = []
    for s in range(2):
        outs.append(np.stack([np.ascontiguousarray(res.results[c]["yT%d" % s].T) for c in range(n_cores)], axis=0))
    return tuple(outs)


def kernel(**inputs):
    inputs = {k: np.asarray(v) for k, v in inputs.items()}
    return run(inputs, FULL_PLAN)
```

```python
import numpy as np
import ml_dtypes
import concourse.bass as bass
import concourse.mybir as mybir
from concourse.ap import AP
from concourse.bass_utils import run_bass_kernel_spmd

F32 = mybir.dt.float32
BF16 = mybir.dt.bfloat16
AF = mybir.ActivationFunctionType
ALU = mybir.AluOpType
AX = mybir.AxisListType

D = 1024
DC = 8
DFF = 2816
FC = 22
DEPTH = 4
EPS = 1e-6
T = 512
N_CORES = 8
DEBUG = False
GDN_STAGE = 9
GDN_SUB = 9
GDN_NLV = 5
GDN_X = 0
LAST = {}


class Res:
    __slots__ = ("name", "w", "r")

    def __init__(self, name=""):
        self.name = name
        self.w = None
        self.r = {}


class Prog:
    ENG = ("pe", "act", "dve", "pool", "sp")

    def __init__(self, nc):
        self.nc = nc
        self.streams = {e: [] for e in self.ENG}
        self.tick = {e: 0 for e in self.ENG}
        self.waited = {e: {} for e in self.ENG}
        self.semnames = list(self.ENG)
        self.dq = {}
        self.dqi = {}
        self.dcount = {}
        for q in ("sp", "pool", "act"):
            self.dq[q] = []
            for i in range(16):
                nm = "d_%s_%d" % (q, i)
                self.semnames.append(nm)
                self.dq[q].append(nm)
                self.dcount[nm] = 0
            self.dqi[q] = 0
        self.last_tok = {}

    def _waits(self, eng, toks):
        out = []
        wd = self.waited[eng]
        for t in toks:
            if t is None:
                continue
            s, v = t
            if wd.get(s, 0) < v:
                wd[s] = v
                out.append((s, v))
        return out

    def _deps(self, eng, reads, writes):
        toks = []
        for r in reads:
            if r.w is not None:
                if r.w[0] == eng and eng == "pe":
                    continue
                toks.append(r.w)
        for w in writes:
            if w.w is not None and w.w[0] != eng:
                toks.append(w.w)
            for s, v in w.r.items():
                if s != eng:
                    toks.append((s, v))
        return toks

    def op(self, eng, fn, reads=(), writes=(), sig=True):
        waits = self._waits(eng, self._deps(eng, reads, writes))
        if sig:
            self.tick[eng] += 1
            tok = (eng, self.tick[eng])
            self.streams[eng].append((waits, fn, (eng, 1)))
        else:
            tok = (eng, self.tick[eng] + 1)
            self.streams[eng].append((waits, fn, None))
        for r in reads:
            if r.r.get(tok[0], 0) < tok[1]:
                r.r[tok[0]] = tok[1]
        for w in writes:
            w.w = tok
            w.r = {}
        self.last_tok[eng] = tok
        return tok

    def dma(self, q, out, in_, reads=(), writes=(), **kw):
        i = self.dqi[q]
        self.dqi[q] = i + 1
        sem = self.dq[q][i % len(self.dq[q])]
        prev = self.dcount[sem]
        toks = self._deps(q, reads, writes)
        if prev:
            toks.append((sem, prev))
        waits = self._waits(q, toks)
        self.dcount[sem] = prev + 16
        tok = (sem, prev + 16)

        def fn(e, out=out, in_=in_, kw=kw):
            return e.dma_start(out=out, in_=in_, **kw)

        self.streams[q].append((waits, fn, (sem, 16)))
        for r in reads:
            if r.r.get(tok[0], 0) < tok[1]:
                r.r[tok[0]] = tok[1]
        for w in writes:
            w.w = tok
            w.r = {}
        return tok

    def barrier(self):
        toks = [(e, self.tick[e]) for e in self.ENG if self.tick[e] > 0]
        toks += [(s, c) for s, c in self.dcount.items() if c > 0]
        for e in self.ENG:
            waits = self._waits(e, toks)
            if waits:
                self.streams[e].append((waits, None, None))

    def emit(self):
        nc = self.nc
        import contextlib
        with contextlib.ExitStack() as es:
            sems = {nm: es.enter_context(nc.semaphore(nm)) for nm in self.semnames}
            block = es.enter_context(nc.Block())
            engobj = {"pe": "tensor", "act": "scalar", "dve": "vector", "pool": "gpsimd", "sp": "sync"}

            def mk(ename):
                stream = self.streams[ename]

                def body(e):
                    for waits, fn, inc in stream:
                        if fn is None:
                            for s, v in waits:
                                e.wait_ge(sems[s], v)
                            continue
                        for s, v in waits[:-1]:
                            e.wait_ge(sems[s], v)
                        ins = fn(e)
                        if waits:
                            ins._wait_ge(sems[waits[-1][0]], waits[-1][1])
                        if inc is not None:
                            ins.then_inc(sems[inc[0]], inc[1])
                return body

            for ename in self.ENG:
                getattr(block, engobj[ename])(mk(ename))


class Arena:
    def __init__(self, nc, nbytes):
        self.t = nc.alloc_sbuf_tensor("arena", [128, nbytes // 2], BF16)
        self.cap = nbytes
        self.off = 0

    def alloc(self, shape_free, dt, parts=128):
        n = 1
        for s in shape_free:
            n *= s
        esz = 4 if dt == F32 else 2
        nb = (n * esz + 63) // 64 * 64
        assert self.off + nb <= self.cap, ("arena overflow", self.off, nb, self.cap)
        v = self.t[0:parts, self.off // 2:(self.off + n * esz) // 2]
        if dt == F32:
            v = v.bitcast(F32)
        self.off += nb
        if len(shape_free) == 2:
            v = v.rearrange("p (a b) -> p a b", a=shape_free[0])
        elif len(shape_free) == 3:
            v = v.rearrange("p (a b c) -> p a b c", a=shape_free[0], b=shape_free[1])
        return v

    def mark(self):
        return self.off

    def reset(self, m):
        self.off = m


def bcast_free(ap, n, axis_pos=1):
    dims = [list(d) for d in ap.ap]
    dims.insert(axis_pos, [0, n])
    return AP(ap.tensor, ap.offset, dims)


class Ctx:
    pass


def build(S_list, plan, debug_out=None):
    nc = bass.Bass("TRN2", target_bir_lowering=False)
    P = Prog(nc)
    nseq = len(S_list)
    C = Ctx()
    C.nc, C.P, C.S = nc, P, S_list

    C.declared = {}

    def din(name, shape, dt=F32):
        if name not in C.declared:
            C.declared[name] = nc.dram_tensor(name, list(shape), dt, kind="ExternalInput").ap()
        return C.declared[name]

    def dscr(name, shape, dt=F32):
        return nc.dram_tensor(name, list(shape), dt, kind=("ExternalOutput" if DEBUG else "Internal")).ap()

    xin = [din("xT%d" % s, [D, S_list[s]]) for s in range(nseq)]
    yout = [nc.dram_tensor("yT%d" % s, [D, S_list[s]], F32, kind="ExternalOutput").ap() for s in range(nseq)]
    cT = din("cT", [128, DC * nseq])
    layers = sorted(set(it[1] for it in plan)) or [0]
    w_ada = din("w_ada", [len(layers), D, 9 * D])
    b_adaT = din("b_adaT", [128, DEPTH * 72])
    npreT = din("npreT", [128, DEPTH * 3 * DC])
    npostT = din("npostT", [128, DEPTH * 3 * DC])
    C.W = {}
    ar = Arena(nc, 212000)
    C.ar = ar
    ones_bf = ar.alloc([128], BF16)
    modT = ar.alloc([DEPTH * 72 * nseq], F32)
    Apre = ar.alloc([DEPTH * 3 * DC * nseq], F32)
    Gpost = ar.alloc([DEPTH * 3 * DC * nseq], F32)
    npre_sb = ar.alloc([DEPTH * 3 * DC], F32)
    npost_sb = ar.alloc([DEPTH * 3 * DC], F32)
    bada_sb = ar.alloc([DEPTH * 72], F32)
    sc_sb = ar.alloc([DC * nseq], F32)
    R_const = Res("const")
    psum = [nc.alloc_psum_tensor("ps%d" % i, [128, 512], F32) if hasattr(nc, "alloc_psum_tensor") else None for i in range(8)]
    C.psum = [p[:, :] for p in psum]
    C.psr = [Res("ps%d" % i) for i in range(8)]

    P.op("pool", lambda e: e.memset(ones_bf, 1.0 / D), writes=[R_const])
    P.dma("sp", npre_sb, npreT, writes=[R_const])
    P.dma("sp", npost_sb, npostT, writes=[R_const])
    P.dma("sp", bada_sb, b_adaT, writes=[R_const])
    P.dma("sp", sc_sb, cT, writes=[R_const])
    P.op("act", lambda e: e.activation(out=sc_sb, in_=sc_sb, func=AF.Silu), reads=[R_const], writes=[R_const])

    def modidx(l, j, t, dc, s):
        return ((l * 72 + j * 24 + t * 8 + dc) * nseq) + s

    m0 = ar.mark()
    wblk = [ar.alloc([DC, 1024], F32) for _ in range(2)]
    wres = [Res("wada%d" % i) for i in range(2)]
    R_mod = Res("mod")
    sc3 = sc_sb.rearrange("p (k s) -> p k s", s=nseq)
    blk = 0
    for li, l in enumerate(layers):
        pst = C.psum[0]
        for cb in range(9):
            wb, wr = wblk[blk % 2], wres[blk % 2]
            blk += 1
            src = w_ada[li, :, cb * 1024:(cb + 1) * 1024].rearrange("(k p) n -> p k n", p=128)
            for k in range(DC):
                P.dma("sp", wb[:, k, :], src[:, k, :], writes=[wr])
            for o in range(8):
                oc = cb * 8 + o
                for k in range(DC):
                    P.op("pe", lambda e, wb=wb, k=k, o=o, oc=oc, pst=pst: e.matmul(
                        pst[:, oc * nseq:(oc + 1) * nseq], wb[:, k, o * 128:(o + 1) * 128], sc3[:, k, :],
                        start=(k == 0), stop=(k == DC - 1)),
                        reads=[wr, R_const], writes=[C.psr[0]], sig=(k == DC - 1))
        mo = modT[:, l * 72 * nseq:(l + 1) * 72 * nseq].rearrange("p (o s) -> p o s", s=nseq)
        bb = bcast_free(bada_sb[:, l * 72:(l + 1) * 72], nseq, 2)
        P.op("dve", lambda e, mo=mo, bb=bb, pst=pst: e.tensor_tensor(
            mo, pst[:, 0:72 * nseq].rearrange("p (o s) -> p o s", s=nseq), bb, ALU.add),
            reads=[C.psr[0], R_const], writes=[R_mod])
    for l in layers:
        for j in range(3):
            wgt = 1.0 if j == 1 else 0.5
            base = modidx(l, j, 0, 0, 0)
            sh = modT[:, base:base + 8 * nseq]
            scl = modT[:, base + 8 * nseq:base + 16 * nseq].rearrange("p (c s) -> p c s", s=nseq)
            gat = modT[:, base + 16 * nseq:base + 24 * nseq].rearrange("p (c s) -> p c s", s=nseq)
            a_o = Apre[:, (l * 3 + j) * DC * nseq:(l * 3 + j + 1) * DC * nseq].rearrange("p (c s) -> p c s", s=nseq)
            g_o = Gpost[:, (l * 3 + j) * DC * nseq:(l * 3 + j + 1) * DC * nseq].rearrange("p (c s) -> p c s", s=nseq)
            npb = bcast_free(npre_sb[:, (l * 3 + j) * DC:(l * 3 + j + 1) * DC], nseq, 2)
            nqb = bcast_free(npost_sb[:, (l * 3 + j) * DC:(l * 3 + j + 1) * DC], nseq, 2)
            P.op("dve", lambda e, a_o=a_o, scl=scl, npb=npb: e.scalar_tensor_tensor(
                a_o, scl, 1.0, npb, ALU.add, ALU.mult), reads=[R_mod, R_const], writes=[R_mod])
            P.op("dve", lambda e, g_o=g_o, gat=gat, nqb=nqb, wgt=wgt: e.scalar_tensor_tensor(
                g_o, gat, wgt, nqb, ALU.mult, ALU.mult), reads=[R_mod, R_const], writes=[R_mod])
    P.barrier()
    ar.reset(m0)

    def A_col(l, j, c, s):
        i = ((l * 3 + j) * DC + c) * nseq + s
        return Apre[:, i:i + 1]

    def B_col(l, j, c, s):
        i = modidx(l, j, 0, c, s)
        return modT[:, i:i + 1]

    def G_col(l, j, c, s):
        i = ((l * 3 + j) * DC + c) * nseq + s
        return Gpost[:, i:i + 1]

    C.A_col, C.B_col, C.G_col, C.R_mod, C.R_const, C.ones_bf = A_col, B_col, G_col, R_mod, R_const, ones_bf

    C.xsrc = list(xin)
    C.xdst = list(yout)
    C.xres = [[Res("x%d_%d" % (s, t)) for t in range(S_list[s] // T)] for s in range(nseq)]

    def xview(apx, t):
        return apx[:, t * T:(t + 1) * T].rearrange("(c p) t -> p c t", p=128)
    C.xview = xview

    eps_col = ar.alloc([1], F32)
    P.op("pool", lambda e: e.memset(eps_col, EPS), writes=[R_const])

    def rsqrt_ps(dst, dstr, ps, psr, scale=1.0):
        P.op("act", lambda e: e.activation(out=dst, in_=ps, func=AF.Sqrt, bias=eps_col[0:dst.shape[0], :], scale=scale),
             reads=[psr, R_const], writes=[dstr])
        P.op("dve", lambda e: e.reciprocal(dst, dst), reads=[dstr], writes=[dstr])
    C.rsqrt_ps = rsqrt_ps

    def norm_mod(xt, xr, l, j, s, hT, hr, sq, sqr, rstd, rstdr, tmp2, tmpr, ps_i, tw=()):
        ps, psr = C.psum[ps_i], C.psr[ps_i]
        P.op("act", lambda e: e.activation(out=sq, in_=xt, func=AF.Square), reads=[xr], writes=[sqr])
        for c in range(DC):
            P.op("pe", lambda e, c=c: e.matmul(ps, ones_bf, sq[:, c, :], start=(c == 0), stop=(c == DC - 1)),
                 reads=[sqr, R_const], writes=[psr], sig=(c == DC - 1))
        rsqrt_ps(rstd, rstdr, ps, psr)
        for c in range(DC):
            tb, tr = tmp2[c % 2], tmpr[c % 2]
            P.op("dve", lambda e, c=c, tb=tb: e.tensor_tensor(tb, xt[:, c, :], rstd, ALU.mult),
                 reads=[xr, rstdr], writes=[tr] + list(tw))
            P.op("act", lambda e, c=c, tb=tb: e.activation(out=hT[:, c, :], in_=tb, func=AF.Identity,
                                                           scale=A_col(l, j, c, s), bias=B_col(l, j, c, s)),
                 reads=[tr, R_mod], writes=[hr])
    C.norm_mod = norm_mod

    def post_residual(outT, outr, sq, sqr, xt, xr, l, j, s, rstd, rstdr, ps_i):
        ps, psr = C.psum[ps_i], C.psr[ps_i]
        for c in range(DC):
            P.op("pe", lambda e, c=c: e.matmul(ps, ones_bf, sq[:, c, :], start=(c == 0), stop=(c == DC - 1)),
                 reads=[sqr, R_const], writes=[psr], sig=(c == DC - 1))
        rsqrt_ps(rstd, rstdr, ps, psr)
        P.op("pool", lambda e: e.tensor_tensor(outT, outT, bcast_free(rstd, DC, 1), ALU.mult),
             reads=[outr, rstdr], writes=[outr])
        for c in range(DC):
            P.op("dve", lambda e, c=c: e.scalar_tensor_tensor(outT[:, c, :], outT[:, c, :], G_col(l, j, c, s),
                                                              xt[:, c, :], ALU.mult, ALU.add),
                 reads=[outr, xr, R_mod], writes=[outr])
    C.post_residual = post_residual

    def ffn_pass(l, j):
        jj = 0 if j == 0 else 1
        sub = 0 if j == 0 else 2
        ffn_w_in = din("ffn_w_in", [DEPTH, 2, D, 2 * DFF])
        ffn_w_out = din("ffn_w_out", [DEPTH, 2, DFF, D])
        m = ar.mark()
        Win = ar.alloc([DC, 2 * DFF], BF16)
        Wout = ar.alloc([FC, D], BF16)
        xt = ar.alloc([DC, T], F32)
        hT = ar.alloc([DC, T], BF16)
        act = ar.alloc([FC, T], BF16)
        outT = ar.alloc([DC, T], F32)
        rstd = ar.alloc([T], F32)
        rstd2 = rstd
        tmp2 = [outT[:, 0, :], outT[:, 1, :]]
        sg2 = [ar.alloc([T], BF16) for _ in range(2)]
        sq = act[:, 0:DC, :]
        Rw, Rx, Rh, Ract, Rout, Rr = (Res(n) for n in ("w", "xt", "hT", "act", "outT", "rstd"))
        Rr2 = Rr
        Rt = [Res("t0"), Res("t1")]
        Rsg = [Res("sg0"), Res("sg1")]
        wi = ffn_w_in[l, jj].rearrange("(k p) n -> p k n", p=128)
        wo = ffn_w_out[l, jj].rearrange("(f p) n -> p f n", p=128)
        for k in range(DC):
            P.dma("pool", Win[:, k, :], wi[:, k, :], writes=[Rw])
        for f in range(FC):
            P.dma("pool", Wout[:, f, :], wo[:, f, :], writes=[Rw])
        for s in range(nseq):
            for t in range(S_list[s] // T):
                xres = C.xres[s][t]
                P.dma("sp", xt, xview(C.xsrc[s], t), reads=[xres], writes=[Rx])
                norm_mod(xt, Rx, l, sub, s, hT, Rh, sq, Ract, rstd, Rr, tmp2, Rt, 0, tw=[Rout])
                for f in range(FC):
                    pg, pu = 1 + 2 * (f % 2), 2 + 2 * (f % 2)
                    for (pb, col) in ((pg, f), (pu, FC + f)):
                        for k in range(DC):
                            P.op("pe", lambda e, pb=pb, col=col, k=k: e.matmul(
                                C.psum[pb], Win[:, k, col * 128:(col + 1) * 128], hT[:, k, :],
                                start=(k == 0), stop=(k == DC - 1)),
                                reads=[Rw, Rh], writes=[C.psr[pb]], sig=(k == DC - 1))
                    sg, rsg = sg2[f % 2], Rsg[f % 2]
                    P.op("act", lambda e, sg=sg, pg=pg: e.activation(out=sg, in_=C.psum[pg], func=AF.Silu),
                         reads=[C.psr[pg]], writes=[rsg])
                    P.op("dve", lambda e, sg=sg, pu=pu, f=f: e.tensor_tensor(act[:, f, :], sg, C.psum[pu], ALU.mult),
                         reads=[rsg, C.psr[pu]], writes=[Ract])
                for o in range(DC):
                    pb = 5 + (o % 2)
                    for f in range(FC):
                        P.op("pe", lambda e, pb=pb, o=o, f=f: e.matmul(
                            C.psum[pb], Wout[:, f, o * 128:(o + 1) * 128], act[:, f, :],
                            start=(f == 0), stop=(f == FC - 1)),
                            reads=[Rw, Ract], writes=[C.psr[pb]], sig=(f == FC - 1))
                    P.op("act", lambda e, pb=pb, o=o: e.activation(out=outT[:, o, :], in_=C.psum[pb], func=AF.Copy),
                         reads=[C.psr[pb]], writes=[Rout])
                    P.op("dve", lambda e, pb=pb, o=o: e.tensor_tensor(hT[:, o, :], outT[:, o, :], outT[:, o, :], ALU.mult),
                         reads=[Rout], writes=[Rh])
                post_residual(outT, Rout, hT, Rh, xt, Rx, l, sub, s, rstd2, Rr2, 7)
                P.dma("pool", xview(C.xdst[s], t), outT, reads=[Rout], writes=[xres])
            C.xsrc[s] = C.xdst[s]
        P.barrier()
        ar.reset(m)
    C.ffn_pass = ffn_pass


    def mm(ps_ap, psr, pairs, reads):
        n = len(pairs)
        for idx, (l_, r_) in enumerate(pairs):
            P.op("pe", lambda e, l_=l_, r_=r_, idx=idx: e.matmul(ps_ap, l_, r_, start=(idx == 0), stop=(idx == n - 1)),
                 reads=reads, writes=[psr], sig=(idx == n - 1))
    C.mm = mm
    Smax = max(S_list)
    ones_f = ar.alloc([128], F32)
    P.op("pool", lambda e: e.memset(ones_f, 1.0), writes=[R_const])

    def mixer_post(l, wname, wshape, widx, oscr, bias_col=None):
        wdr = din(wname, wshape)
        m = ar.mark()
        Wo = ar.alloc([DC, D], BF16)
        xt = ar.alloc([DC, T], F32)
        ot = ar.alloc([DC, T], BF16)
        sq = ar.alloc([DC, T], BF16)
        outT = ar.alloc([DC, T], F32)
        rstd = ar.alloc([T], F32)
        Rw, Rx, Ro, Rsq, Rout, Rr = (Res(n) for n in ("w", "xt", "ot", "sq", "outT", "rstd"))
        wv = wdr[widx].rearrange("(k p) n -> p k n", p=128)
        for k in range(DC):
            P.dma("pool", Wo[:, k, :], wv[:, k, :], writes=[Rw])
        for s in range(nseq):
            for t in range(S_list[s] // T):
                xres = C.xres[s][t]
                P.dma("sp", xt, xview(C.xsrc[s], t), reads=[xres], writes=[Rx])
                P.dma("sp", ot, xview(oscr[s], t), reads=[C.scr_res], writes=[Ro])
                for o in range(DC):
                    pb = 1 + (o % 2)
                    mm(C.psum[pb], C.psr[pb], [(Wo[:, k, o * 128:(o + 1) * 128], ot[:, k, :]) for k in range(DC)], [Rw, Ro])
                    if bias_col is None:
                        P.op("act", lambda e, pb=pb, o=o: e.activation(out=outT[:, o, :], in_=C.psum[pb], func=AF.Copy),
                             reads=[C.psr[pb]], writes=[Rout])
                    else:
                        P.op("act", lambda e, pb=pb, o=o: e.activation(out=outT[:, o, :], in_=C.psum[pb], func=AF.Identity,
                                                                       bias=bias_col(o), scale=1.0),
                             reads=[C.psr[pb], R_const], writes=[Rout])
                    P.op("dve", lambda e, o=o: e.tensor_tensor(sq[:, o, :], outT[:, o, :], outT[:, o, :], ALU.mult),
                         reads=[Rout], writes=[Rsq])
                post_residual(outT, Rout, sq, Rsq, xt, Rx, l, 1, s, rstd, Rr, 7)
                P.dma("pool", xview(C.xdst[s], t), outT, reads=[Rout], writes=[xres])
            C.xsrc[s] = C.xdst[s]
        P.barrier()
        ar.reset(m)
    C.scr_res = Res("scratch")

    def dbg_dump(name, sb, rlist, dt):
        if not DEBUG:
            return
        shp = list(sb.shape)
        dst = nc.dram_tensor("dbg_" + name, shp, dt, kind="ExternalOutput").ap()
        P.barrier()
        P.dma("sp", dst, sb, reads=rlist)
        P.barrier()
    C.dbg_dump = dbg_dump

    MLA_SCALE = 192.0 ** -0.5

    def mla_layer(l):
        i = l // 3
        wd = din("mla_w_down", [2, D, 704])
        wdsw = din("mla_w_down_sw", [2, D, 64])
        qnT = din("mla_qnT", [128, 2 * 3])
        kvnT = din("mla_kvnT", [128, 2 * 2])
        wuq = din("mla_w_uq", [2, 384, 1536])
        wuqsw = din("mla_w_uq_sw", [2, 384, 512])
        wukv = din("mla_w_ukv", [2, 256, 2048])
        cos2 = din("rope_cos2", [64, Smax])
        sin2s = din("rope_sin2s", [64, Smax])
        oscr = [dscr("mla_o%d_%d" % (l, s), [D, S_list[s]], BF16) for s in range(nseq)]
        def seq_body(s):
            S = S_list[s]
            NT = S // T
            m = ar.mark()
            cqT = ar.alloc([3, S], BF16)
            ckvT = ar.alloc([2, S], BF16)
            krT = ar.alloc([S], BF16)
            Rcq, Rckv, Rkr = Res("cq"), Res("ckv"), Res("kr")
            m1 = ar.mark()
            def pre():
                Wd = ar.alloc([DC, 704], BF16)
                Wdsw = ar.alloc([DC, 64], BF16)
                nrm = ar.alloc([8], F32)
                xt = ar.alloc([DC, T], F32)
                hT = ar.alloc([DC, T], BF16)
                sq = ar.alloc([DC, T], BF16)
                rstd = ar.alloc([T], F32)
                tmp2 = [ar.alloc([T], F32) for _ in range(2)]
                dn = ar.alloc([3, T], F32)
                dsq = ar.alloc([3, T], BF16)
                rs2 = ar.alloc([T], F32)
                cs = ar.alloc([2, T], F32)
                rt = ar.alloc([2, T], F32)
                Rw, Rx, Rh, Rsq, Rr, Rdn, Rdsq, Rrs2, Rcs, Rrt = (Res(n) for n in "w x h sq r dn dsq rs2 cs rt".split())
                Rt = [Res("t0"), Res("t1")]
                wv = wd[i].rearrange("(k p) n -> p k n", p=128)
                wsv = wdsw[i].rearrange("(k p) n -> p k n", p=128)
                for k in range(DC):
                    P.dma("pool", Wd[:, k, :], wv[:, k, :], writes=[Rw])
                    P.dma("pool", Wdsw[:, k, :], wsv[:, k, :], writes=[Rw])
                P.dma("sp", nrm[:, 0:3], qnT[:, i * 3:(i + 1) * 3], writes=[Rw])
                P.dma("sp", nrm[:, 3:5], kvnT[:, i * 2:(i + 1) * 2], writes=[Rw])
                for t in range(NT):
                    tsl = slice(t * T, (t + 1) * T)
                    P.dma("sp", xt, xview(C.xsrc[s], t), reads=[C.xres[s][t]], writes=[Rx])
                    P.dma("sp", cs[0:64, 0, :], cos2[:, tsl], writes=[Rcs])
                    P.dma("sp", cs[0:64, 1, :], sin2s[:, tsl], writes=[Rcs])
                    norm_mod(xt, Rx, l, 1, s, hT, Rh, sq, Rsq, rstd, Rr, tmp2, Rt, 0)
                    for (nch, col0, dst, rdst, ncol0, fan) in ((3, 0, cqT, Rcq, 0, 384.0), (2, 384, ckvT, Rckv, 3, 256.0)):
                        for c in range(nch):
                            pb = 1 + (c % 2)
                            mm(C.psum[pb], C.psr[pb], [(Wd[:, k, col0 + c * 128:col0 + (c + 1) * 128], hT[:, k, :]) for k in range(DC)], [Rw, Rh])
                            P.op("act", lambda e, pb=pb, c=c: e.activation(out=dn[:, c, :], in_=C.psum[pb], func=AF.Copy),
                                 reads=[C.psr[pb]], writes=[Rdn])
                            P.op("dve", lambda e, c=c: e.tensor_tensor(dsq[:, c, :], dn[:, c, :], dn[:, c, :], ALU.mult),
                                 reads=[Rdn], writes=[Rdsq])
                        mm(C.psum[3], C.psr[3], [(ones_bf, dsq[:, c, :]) for c in range(nch)], [Rdsq, R_const])
                        rsqrt_ps(rs2, Rrs2, C.psum[3], C.psr[3], scale=D / fan)
                        if t == 0 and s == 0:
                            dbg_dump("dn%d" % nch, dn, [Rdn], F32)
                            dbg_dump("rs2%d" % nch, rs2, [Rrs2], F32)
                            dbg_dump("nrm%d" % nch, nrm, [Rw], F32)
                            dbg_dump("hT%d" % nch, hT, [Rh], BF16)
                        for c in range(nch):
                            P.op("dve", lambda e, c=c, dst=dst, ncol0=ncol0, tsl=tsl: e.scalar_tensor_tensor(
                                dst[:, c, tsl], dn[:, c, :], nrm[:, ncol0 + c:ncol0 + c + 1], rs2, ALU.mult, ALU.mult),
                                reads=[Rdn, Rrs2, Rw], writes=[rdst])
                    mm(C.psum[4][0:64, :], C.psr[4], [(Wd[:, k, 640:704], hT[:, k, :]) for k in range(DC)], [Rw, Rh])
                    mm(C.psum[5][0:64, :], C.psr[5], [(Wdsw[:, k, :], hT[:, k, :]) for k in range(DC)], [Rw, Rh])
                    P.op("dve", lambda e: e.tensor_tensor(rt[0:64, 0, :], C.psum[4][0:64, :], cs[0:64, 0, :], ALU.mult),
                         reads=[C.psr[4], Rcs], writes=[Rrt])
                    P.op("dve", lambda e: e.tensor_tensor(rt[0:64, 1, :], C.psum[5][0:64, :], cs[0:64, 1, :], ALU.mult),
                         reads=[C.psr[5], Rcs], writes=[Rrt])
                    P.op("pool", lambda e, tsl=tsl: e.tensor_tensor(krT[0:64, tsl], rt[0:64, 0, :], rt[0:64, 1, :], ALU.add),
                         reads=[Rrt], writes=[Rkr])
                    if t == 0 and s == 0:
                        dbg_dump("rt", rt, [Rrt], F32)
                        dbg_dump("cs", cs, [Rcs], F32)
                if DEBUG:
                    C.dbgR = dict(Rdn=None)
            pre()
            dbg_dump("cqT%d" % s, cqT, [Rcq], BF16)
            dbg_dump("ckvT%d" % s, ckvT, [Rckv], BF16)
            dbg_dump("krT%d" % s, krT[0:64, :], [Rkr], BF16)
            P.barrier()
            ar.reset(m1)
            def attn():
                Wuq = ar.alloc([3, 1536], BF16)
                Wuqs = ar.alloc([3, 512], BF16)
                Wukv = ar.alloc([2, 2048], BF16)
                KT = ar.alloc([S], BF16)
                QT = ar.alloc([S], BF16)
                qrT = ar.alloc([S], BF16)
                Vtm = ar.alloc([S // 128, 128], BF16)
                PT = [ar.alloc([T], BF16) for _ in range(3)]
                Pacc = ar.alloc([T], F32)
                rcp = ar.alloc([T], F32)
                osb = [ar.alloc([T], BF16) for _ in range(2)]
                csq = ar.alloc([2, S], F32) if False else None
                cs = ar.alloc([2, T], F32)
                rt = ar.alloc([2, T], F32)
                Rw, RK, RQ, Rqr, RV, Racc, Rrcp, Rcs, Rrt = (Res(n) for n in "w K Q qr V acc rcp cs rt".split())
                RPT = [Res("pt%d" % k) for k in range(3)]
                Ros = [Res("os0"), Res("os1")]
                for (dst, srcw, nk) in ((Wuq, wuq, 3), (Wuqs, wuqsw, 3), (Wukv, wukv, 2)):
                    v = srcw[i].rearrange("(k p) n -> p k n", p=128)
                    for k in range(nk):
                        P.dma("pool", dst[:, k, :], v[:, k, :], writes=[Rw])
                ocnt = 0
                for hd in range(8):
                    for t in range(NT):
                        tsl = slice(t * T, (t + 1) * T)
                        mm(C.psum[0], C.psr[0], [(Wukv[:, k, hd * 256:hd * 256 + 128], ckvT[:, k, tsl]) for k in range(2)], [Rw, Rckv])
                        P.op("act", lambda e, tsl=tsl: e.activation(out=KT[:, tsl], in_=C.psum[0], func=AF.Copy),
                             reads=[C.psr[0]], writes=[RK])
                        mm(C.psum[1], C.psr[1], [(Wuq[:, k, hd * 192:hd * 192 + 128], cqT[:, k, tsl]) for k in range(3)], [Rw, Rcq])
                        P.op("dve", lambda e, tsl=tsl: e.tensor_scalar(QT[:, tsl], C.psum[1], MLA_SCALE, None, ALU.mult),
                             reads=[C.psr[1]], writes=[RQ])
                        mm(C.psum[2][0:64, :], C.psr[2], [(Wuq[:, k, hd * 192 + 128:hd * 192 + 192], cqT[:, k, tsl]) for k in range(3)], [Rw, Rcq])
                        mm(C.psum[3][0:64, :], C.psr[3], [(Wuqs[:, k, hd * 64:hd * 64 + 64], cqT[:, k, tsl]) for k in range(3)], [Rw, Rcq])
                        P.dma("sp", cs[0:64, 0, :], cos2[:, tsl], writes=[Rcs])
                        P.dma("sp", cs[0:64, 1, :], sin2s[:, tsl], writes=[Rcs])
                        P.op("dve", lambda e: e.scalar_tensor_tensor(rt[0:64, 0, :], C.psum[2][0:64, :], MLA_SCALE, cs[0:64, 0, :], ALU.mult, ALU.mult),
                             reads=[C.psr[2], Rcs], writes=[Rrt])
                        P.op("dve", lambda e: e.scalar_tensor_tensor(rt[0:64, 1, :], C.psum[3][0:64, :], MLA_SCALE, cs[0:64, 1, :], ALU.mult, ALU.mult),
                             reads=[C.psr[3], Rcs], writes=[Rrt])
                        P.op("pool", lambda e, tsl=tsl: e.tensor_tensor(qrT[0:64, tsl], rt[0:64, 0, :], rt[0:64, 1, :], ALU.add),
                             reads=[Rrt], writes=[Rqr])
                        for q4 in range(4):
                            tt = t * 4 + q4
                            mm(C.psum[4][:, q4 * 128:(q4 + 1) * 128], C.psr[4],
                               [(ckvT[:, k, tt * 128:(tt + 1) * 128], Wukv[:, k, hd * 256 + 128:hd * 256 + 256]) for k in range(2)], [Rw, Rckv])
                        P.op("act", lambda e, t=t: e.activation(out=Vtm[:, t * 4:(t + 1) * 4, :], in_=C.psum[4].rearrange("p (a b) -> p a b", a=4), func=AF.Copy),
                             reads=[C.psr[4]], writes=[RV])
                    if hd == 0:
                        dbg_dump("KT%d" % s, KT, [RK], BF16)
                        dbg_dump("QT%d" % s, QT, [RQ], BF16)
                        dbg_dump("qrT%d" % s, qrT[0:64, :], [Rqr], BF16)
                        dbg_dump("Vtm%d" % s, Vtm, [RV], BF16)
                    nkt = S // 128
                    for qt in range(NT):
                        qsl = slice(qt * T, (qt + 1) * T)
                        for kt in range(nkt):
                            ksl = slice(kt * 128, (kt + 1) * 128)
                            pb = kt % 3
                            mm(C.psum[pb], C.psr[pb], [(KT[:, ksl], QT[:, qsl]), (krT[0:64, ksl], qrT[0:64, qsl])], [RK, RQ, Rkr, Rqr])
                            pt, rpt = PT[kt % 3], RPT[kt % 3]
                            P.op("act", lambda e, pb=pb, pt=pt: e.activation(out=pt, in_=C.psum[pb], func=AF.Exp),
                                 reads=[C.psr[pb]], writes=[rpt])
                            P.op("pe", lambda e, kt=kt, pt=pt, nkt=nkt: e.matmul(C.psum[5], Vtm[:, kt, :], pt, start=(kt == 0), stop=(kt == nkt - 1)),
                                 reads=[RV, rpt], writes=[C.psr[5]], sig=True)
                            if kt == 0:
                                P.op("dve", lambda e, pt=pt: e.tensor_copy(Pacc, pt), reads=[rpt], writes=[Racc])
                            else:
                                P.op("dve", lambda e, pt=pt: e.tensor_tensor(Pacc, Pacc, pt, ALU.add), reads=[rpt, Racc], writes=[Racc])
                        mm(C.psum[6], C.psr[6], [(ones_f, Pacc)], [Racc, R_const])
                        P.op("dve", lambda e: e.reciprocal(rcp, C.psum[6]), reads=[C.psr[6]], writes=[Rrcp])
                        ob, rob = osb[ocnt % 2], Ros[ocnt % 2]
                        ocnt += 1
                        P.op("dve", lambda e, ob=ob: e.tensor_tensor(ob, C.psum[5], rcp, ALU.mult), reads=[C.psr[5], Rrcp], writes=[rob])
                        P.dma("pool", oscr[s][hd * 128:(hd + 1) * 128, qsl], ob, reads=[rob], writes=[C.scr_res])
            attn()
            P.barrier()
            ar.reset(m)
        for s in range(nseq):
            seq_body(s)
        mixer_post(l, "mla_w_out", [2, D, D], i, oscr)
    C.mla_layer = mla_layer


    def fnet_layer(l):
        i = l // 3
        cs4d = din("fn_cs4", [128, 512])
        bT = din("fnet_bT", [128, DC])
        hbuf = [dscr("fn_h%d" % s_, [D, S_list[s_]], BF16) for s_ in range(nseq)]
        fbuf = [dscr("fn_f%d" % s_, [D, S_list[s_]], BF16) for s_ in range(nseq)]
        gd = [dscr("fn_g%d" % s_, [2, S_list[s_] // 128, 128, D], BF16) for s_ in range(nseq)]

        def pass1():
            m = ar.mark()
            xt = ar.alloc([DC, T], F32)
            hT = ar.alloc([DC, T], BF16)
            sq = ar.alloc([DC, T], BF16)
            rstd = ar.alloc([T], F32)
            tmp2 = [ar.alloc([T], F32) for _ in range(2)]
            Rx, Rh, Rsq, Rr = (Res(n) for n in "x h sq r".split())
            Rt = [Res("t0"), Res("t1")]
            for s_ in range(nseq):
                for t in range(S_list[s_] // T):
                    P.dma("sp", xt, xview(C.xsrc[s_], t), reads=[C.xres[s_][t]], writes=[Rx])
                    norm_mod(xt, Rx, l, 1, s_, hT, Rh, sq, Rsq, rstd, Rr, tmp2, Rt, 0)
                    P.dma("pool", xview(hbuf[s_], t), hT, reads=[Rh], writes=[C.scr_res])
            P.barrier()
            ar.reset(m)
        pass1()

        def pass23(s_):
            S = S_list[s_]
            N2 = S // 128
            SB = min(8, N2)
            twd = din("fn_tw%d" % N2, [128, N2 * 256])
            wd_ = din("fn_w%d" % N2, [2 * N2, N2])
            m = ar.mark()
            CS4 = ar.alloc([512], BF16)
            TW = ar.alloc([N2, 2, 128], BF16)
            WCS = ar.alloc([N2], BF16)
            Rc = Res("fc")
            P.dma("pool", CS4, cs4d, writes=[Rc])
            P.dma("pool", TW, twd.rearrange("p (a b c) -> p a b c", a=N2, b=2), writes=[Rc])
            P.dma("pool", WCS[0:2 * N2, :], wd_, writes=[Rc])
            m2 = ar.mark()

            def stage_ab():
                hg2 = [ar.alloc([S], BF16) for _ in range(2)]
                Rhg = [Res("hg0"), Res("hg1")]
                zb = [ar.alloc([512], BF16) for _ in range(2)]
                Rz = [Res("z0"), Res("z1")]
                gs2 = [ar.alloc([SB, 2, 128], BF16) for _ in range(2)]
                Rgs = [Res("gs0"), Res("gs1")]
                cnt = 0
                for g in range(8):
                    hg, rhg = hg2[g % 2], Rhg[g % 2]
                    P.dma("sp", hg, hbuf[s_][g * 128:(g + 1) * 128, :], reads=[C.scr_res], writes=[rhg])
                    hv = hg.rearrange("p (s1 n2) -> p n2 s1", n2=N2)
                    for sb in range(N2 // SB):
                        gs, rgs = gs2[cnt % 2], Rgs[cnt % 2]
                        cnt += 1
                        for j in range(SB):
                            s2 = sb * SB + j
                            pa, pb = j % 2, 2 + (j % 2)
                            z, rz = zb[j % 2], Rz[j % 2]
                            mm(C.psum[pa], C.psr[pa], [(hv[:, s2, :], CS4)], [rhg, Rc])
                            P.op("act", lambda e, z=z, pa=pa: e.activation(out=z, in_=C.psum[pa], func=AF.Copy),
                                 reads=[C.psr[pa]], writes=[rz])
                            mm(C.psum[pb][:, 0:256], C.psr[pb], [(TW[:, s2, 0, :], z[:, 0:256]), (TW[:, s2, 1, :], z[:, 256:512])], [rz, Rc])
                            P.op("dve", lambda e, gs=gs, j=j, pb=pb: e.tensor_copy(
                                gs[:, j, :, :], C.psum[pb][:, 0:256].rearrange("p (r m) -> p r m", r=2)),
                                reads=[C.psr[pb]], writes=[rgs])
                        for ri in range(2):
                            dst = gd[s_][ri, sb * SB:(sb + 1) * SB, :, g * 128:(g + 1) * 128].rearrange("n k m -> k n m")
                            P.dma("pool", dst, gs[:, :, ri, :], reads=[rgs], writes=[C.scr_res])
                P.barrier()
            stage_ab()
            ar.reset(m2)

            def stage_c():
                fT = ar.alloc([DC, S], BF16)
                Rf = Res("fT")
                KB = 8
                g2 = [ar.alloc([KB, D], BF16) for _ in range(2)]
                Rg2 = [Res("g20"), Res("g21")]
                for kb in range(128 // KB):
                    G2, rg2 = g2[kb % 2], Rg2[kb % 2]
                    for ri in range(2):
                        P.dma("sp", G2[ri * N2:(ri + 1) * N2, :, :], gd[s_][ri, :, kb * KB:(kb + 1) * KB, :],
                              reads=[C.scr_res], writes=[rg2])
                    for g in range(8):
                        pb = g % 4
                        for j in range(KB):
                            mm(C.psum[pb][:, j * N2:(j + 1) * N2], C.psr[pb],
                               [(G2[0:2 * N2, j, g * 128:(g + 1) * 128], WCS[0:2 * N2, :])], [rg2, Rc])
                        dstv = fT[:, g, :].rearrange("p (k2 k1) -> p k1 k2", k1=128)[:, kb * KB:(kb + 1) * KB, :]
                        srcv = C.psum[pb][:, 0:KB * N2].rearrange("p (a b) -> p a b", a=KB)
                        if g % 2 == 0:
                            P.op("act", lambda e, dstv=dstv, srcv=srcv: e.activation(out=dstv, in_=srcv, func=AF.Copy),
                                 reads=[C.psr[pb]], writes=[Rf])
                        else:
                            P.op("dve", lambda e, dstv=dstv, srcv=srcv: e.tensor_copy(dstv, srcv),
                                 reads=[C.psr[pb]], writes=[Rf])
                for c in range(DC):
                    P.dma("pool", fbuf[s_][c * 128:(c + 1) * 128, :], fT[:, c, :], reads=[Rf], writes=[C.scr_res])
                P.barrier()
            stage_c()
            ar.reset(m)
        for s_ in range(nseq):
            pass23(s_)
        mb = ar.mark()
        bsb = ar.alloc([DC], F32)
        P.dma("sp", bsb, bT, writes=[R_const])
        mixer_post(l, "fnet_w_out", [1, D, D], i, fbuf, bias_col=lambda o: bsb[:, o:o + 1])
        ar.reset(mb)
    C.fnet_layer = fnet_layer


    def gdn_layer(l):
        i = l // 3
        w_in = din("gdn_w_in", [1, D, 4128])
        convT = din("gdn_convT", [128, 24 * 5])
        gcon = din("gdn_consts", [64, 64 * 7 + 256])
        gab = din("gdn_ab", [1, 32])
        onT = din("gdn_onT", [128, 1])
        pj = [dscr("gd_pj%d" % s_, [4096, S_list[s_]], BF16) for s_ in range(nseq)]
        ab = [dscr("gd_ab%d" % s_, [S_list[s_], 32], F32) for s_ in range(nseq)]
        ofw = [dscr("gd_of%d" % s_, [8, S_list[s_], 128], F32) for s_ in range(nseq)]
        oscr = [dscr("gd_o%d" % s_, [D, S_list[s_]], BF16) for s_ in range(nseq)]
        bank = [0]

        def nb():
            bank[0] = (bank[0] + 1) % 8
            return bank[0]

        def pass1():
            m = ar.mark()
            Win = ar.alloc([DC, 4128], BF16)
            xt = ar.alloc([DC, T], F32)
            hT = ar.alloc([DC, T], BF16)
            sq = ar.alloc([DC, T], BF16)
            rstd = ar.alloc([T], F32)
            tmp2 = [ar.alloc([T], F32) for _ in range(2)]
            pb4 = [ar.alloc([4, T], BF16) for _ in range(2)]
            abt = ar.alloc([4, 32], F32)
            Rw, Rx, Rh, Rsq, Rr, Rab = (Res(n) for n in "w x h sq r ab".split())
            Rt = [Res("t0"), Res("t1")]
            Rp4 = [Res("p0"), Res("p1")]
            wv = w_in[i].rearrange("(k p) n -> p k n", p=128)
            for k in range(DC):
                P.dma("pool", Win[:, k, :], wv[:, k, :], writes=[Rw])
            cnt = 0
            for s_ in range(nseq):
                for t in range(S_list[s_] // T):
                    tsl = slice(t * T, (t + 1) * T)
                    P.dma("sp", xt, xview(C.xsrc[s_], t), reads=[C.xres[s_][t]], writes=[Rx])
                    norm_mod(xt, Rx, l, 1, s_, hT, Rh, sq, Rsq, rstd, Rr, tmp2, Rt, 0)
                    for c4 in range(8):
                        buf, rb = pb4[cnt % 2], Rp4[cnt % 2]
                        cnt += 1
                        for cc in range(4):
                            ch = c4 * 4 + cc
                            pb = 1 + (ch % 4)
                            mm(C.psum[pb], C.psr[pb], [(Win[:, k, ch * 128:(ch + 1) * 128], hT[:, k, :]) for k in range(DC)], [Rw, Rh])
                            if cc % 2 == 0:
                                P.op("act", lambda e, buf=buf, cc=cc, pb=pb: e.activation(out=buf[:, cc, :], in_=C.psum[pb], func=AF.Copy),
                                     reads=[C.psr[pb]], writes=[rb])
                            else:
                                P.op("dve", lambda e, buf=buf, cc=cc, pb=pb: e.tensor_copy(buf[:, cc, :], C.psum[pb]),
                                     reads=[C.psr[pb]], writes=[rb])
                        P.dma("pool", pj[s_][c4 * 512:(c4 + 1) * 512, tsl].rearrange("(c p) t -> p c t", p=128), buf, reads=[rb], writes=[C.scr_res])
                    for tt in range(4):
                        mm(C.psum[6][:, tt * 32:(tt + 1) * 32], C.psr[6],
                           [(hT[:, k, tt * 128:(tt + 1) * 128], Win[:, k, 4096:4128]) for k in range(DC)], [Rw, Rh])
                    P.op("act", lambda e: e.activation(out=abt, in_=C.psum[6][:, 0:128].rearrange("p (a b) -> p a b", a=4), func=AF.Copy),
                         reads=[C.psr[6]], writes=[Rab])
                    P.dma("pool", ab[s_][tsl, :].rearrange("(a p) c -> p a c", p=128), abt, reads=[Rab], writes=[C.scr_res])
            P.barrier()
            ar.reset(m)
        pass1()

        def core(s_):
            S = S_list[s_]
            N = S // 64
            m = ar.mark()
            GC = ar.alloc([64 * 7 + 256], F32)
            I64b = ar.alloc([64], BF16)
            I128b = ar.alloc([128], BF16)
            cvw = ar.alloc([24, 5], F32)
            onw = ar.alloc([1], F32)
            one_c = ar.alloc([1], F32)
            abc = ar.alloc([32], F32)
            Rc = Res("gc")
            P.dma("sp", GC[0:64, :], gcon, writes=[Rc])
            P.dma("sp", cvw, convT.rearrange("p (a b) -> p a b", a=24), writes=[Rc])
            P.dma("sp", onw, onT, writes=[Rc])
            P.dma("sp", abc[0:64, :], gab.partition_broadcast(64), writes=[Rc])
            P.op("pool", lambda e: e.memset(one_c, 1.0), writes=[Rc])
            TRI = [GC[0:64, 0:64], GC[0:64, 64:128]]
            NEGI = [GC[0:64, 128:192], GC[0:64, 256:320]]
            NEGS = [GC[0:64, 192:256], GC[0:64, 320:384]]
            I64f = GC[0:64, 384:448]
            SEL = [GC[0:64, 448:576], GC[0:64, 576:704]]
            P.op("dve", lambda e: e.tensor_copy(I64b[0:64, :], I64f), reads=[Rc], writes=[Rc])
            P.op("pool", lambda e: e.memset(I128b, 0.0), writes=[Rc])
            P.op("pool", lambda e: e.affine_select(out=I128b, in_=I128b, pattern=[[-1, 128]], compare_op=ALU.not_equal,
                                                   fill=1.0, base=0, channel_multiplier=1), reads=[Rc], writes=[Rc])
            P.op("act", lambda e: e.activation(out=abc[0:64, 0:16], in_=abc[0:64, 0:16], func=AF.Exp), reads=[Rc], writes=[Rc])
            P.op("dve", lambda e: e.tensor_scalar(abc[0:64, 0:16], abc[0:64, 0:16], -1.0, None, ALU.mult), reads=[Rc], writes=[Rc])
            GN = ("beta", "gc", "ngc", "gcb", "egc", "bege", "ekd")
            gsc = dscr("gd_gs%d" % s_, [len(GN), 16, 64, N], F32)
            eglsc = dscr("gd_egl%d" % s_, [16, 128, N], F32)
            Rg = Res("gates")

            def gates():
                mg = ar.mark()
                G = {nm: ar.alloc([16, N], F32) for nm in ("g",) + GN}
                EGL = ar.alloc([16, N], F32)
                abs_ = ar.alloc([N, 32], F32)
                P.dma("sp", abs_[0:64], ab[s_].rearrange("(n t) c -> t n c", t=64), reads=[C.scr_res], writes=[Rg])
                a_ = abs_[0:64, :, 0:16].rearrange("p n c -> p c n")
                b_ = abs_[0:64, :, 16:32].rearrange("p n c -> p c n")
                g, beta, gc, ngc, gcb, egc, bege, ekd = (G[k][0:64] for k in ("g", "beta", "gc", "ngc", "gcb", "egc", "bege", "ekd"))
                P.op("dve", lambda e: e.tensor_tensor(g, a_, bcast_free(abc[0:64, 16:32], N, 2), ALU.add), reads=[Rg, Rc], writes=[Rg])
                P.op("act", lambda e: e.activation(out=g, in_=g, func=AF.Exp), reads=[Rg], writes=[Rg])
                P.op("act", lambda e: e.activation(out=g, in_=g, func=AF.Ln, bias=one_c[0:64, :]), reads=[Rg, Rc], writes=[Rg])
                P.op("dve", lambda e: e.tensor_tensor(g, g, bcast_free(abc[0:64, 0:16], N, 2), ALU.mult), reads=[Rg, Rc], writes=[Rg])
                P.op("act", lambda e: e.activation(out=beta, in_=b_, func=AF.Sigmoid), reads=[Rg], writes=[Rg])
                P.op("act", lambda e: e.activation(out=gcb, in_=beta, func=AF.Ln), reads=[Rg], writes=[Rg])
                NBK = 32
                for d_ in range(2):
                    for n0 in range(0, N, NBK):
                        n1 = min(N, n0 + NBK)
                        w = (n1 - n0) * 8
                        pb = nb()
                        mm(C.psum[pb][0:64, 0:w], C.psr[pb], [(TRI[d_], g[:, d_ * 8:(d_ + 1) * 8, n0:n1])], [Rg, Rc])
                        P.op("act", lambda e, pb=pb, w=w, n0=n0, n1=n1, d_=d_: e.activation(
                            out=gc[:, d_ * 8:(d_ + 1) * 8, n0:n1], in_=C.psum[pb][0:64, 0:w].rearrange("p (a b) -> p a b", a=8), func=AF.Copy),
                            reads=[C.psr[pb]], writes=[Rg])
                        pb2 = nb()
                        mm(C.psum[pb2][:, 0:w], C.psr[pb2], [(SEL[d_], gc[:, d_ * 8:(d_ + 1) * 8, n0:n1])], [Rg, Rc])
                        P.op("act", lambda e, pb2=pb2, w=w, n0=n0, n1=n1, d_=d_: e.activation(
                            out=EGL[:, d_ * 8:(d_ + 1) * 8, n0:n1], in_=C.psum[pb2][:, 0:w].rearrange("p (a b) -> p a b", a=8), func=AF.Exp),
                            reads=[C.psr[pb2]], writes=[Rg])
                        P.op("dve", lambda e, pb2=pb2, w=w, n0=n0, n1=n1, d_=d_: e.tensor_tensor(
                            ekd[:, d_ * 8:(d_ + 1) * 8, n0:n1], C.psum[pb2][0:64, 0:w].rearrange("p (a b) -> p a b", a=8),
                            gc[:, d_ * 8:(d_ + 1) * 8, n0:n1], ALU.subtract),
                            reads=[C.psr[pb2], Rg], writes=[Rg])
                P.op("act", lambda e: e.activation(out=ekd, in_=ekd, func=AF.Exp), reads=[Rg], writes=[Rg])
                P.op("act", lambda e: e.activation(out=egc, in_=gc, func=AF.Exp), reads=[Rg], writes=[Rg])
                P.op("dve", lambda e: e.tensor_tensor(bege, beta, egc, ALU.mult), reads=[Rg], writes=[Rg])
                P.op("dve", lambda e: e.tensor_tensor(gcb, gcb, gc, ALU.add), reads=[Rg], writes=[Rg])
                P.op("dve", lambda e: e.tensor_scalar(ngc, gc, -1.0, None, ALU.mult), reads=[Rg], writes=[Rg])
                for qi, nm in enumerate(GN):
                    P.dma("pool", gsc[qi].rearrange("c t n -> t c n"), G[nm][0:64], reads=[Rg], writes=[C.scr_res])
                P.dma("pool", eglsc.rearrange("c t n -> t c n"), EGL, reads=[Rg], writes=[C.scr_res])
                P.barrier()
                ar.reset(mg)
            if GDN_STAGE >= 2:
                gates()

            def head(hd):
                mh = ar.mark()
                QT = ar.alloc([S], BF16)
                KT = ar.alloc([S], BF16)
                Ktm = ar.alloc([N, 128], BF16)
                Vtm = ar.alloc([N, 128], BF16)
                RQ, RK, RKt, RVt = Res("Q"), Res("K"), Res("Kt"), Res("Vt")
                Gh = {nm: ar.alloc([2, N], F32) for nm in GN}
                EGLh = ar.alloc([2, N], F32)
                Rgh = Res("gh")
                for qi, nm in enumerate(GN):
                    for d2 in range(2):
                        P.dma("sp", Gh[nm][0:64, d2, :], gsc[qi, d2 * 8 + hd], reads=[C.scr_res], writes=[Rgh])
                for d2 in range(2):
                    P.dma("sp", EGLh[:, d2, :], eglsc[d2 * 8 + hd], reads=[C.scr_res], writes=[Rgh])

                def prep():
                    mp = ar.mark()
                    raw = ar.alloc([3, S + 4], BF16)
                    VT = ar.alloc([S], BF16)
                    acc = [ar.alloc([T], F32) for _ in range(2)]
                    sl = [ar.alloc([T], F32) for _ in range(2)]
                    sqb = ar.alloc([T], BF16)
                    rs = ar.alloc([T], F32)
                    Rraw, RVT, Rsqb, Rrs = Res("raw"), Res("VT"), Res("sqb"), Res("rs")
                    Racc = [Res("a0"), Res("a1")]
                    Rsl = [Res("s0"), Res("s1")]
                    P.op("pool", lambda e: e.memset(raw[:, :, 0:2], 0.0), writes=[Rraw])
                    P.op("pool", lambda e: e.memset(raw[:, :, S + 2:S + 4], 0.0), writes=[Rraw])
                    for j in range(3):
                        P.dma("sp", raw[:, j, 2:S + 2], pj[s_][j * 1024 + hd * 128:j * 1024 + (hd + 1) * 128, :], reads=[C.scr_res], writes=[Rraw])
                    k_ = 0
                    for j in range(3):
                        for t in range(S // T):
                            a, ra = acc[k_ % 2], Racc[k_ % 2]
                            so, rso = sl[k_ % 2], Rsl[k_ % 2]
                            k_ += 1
                            tsl = slice(t * T, (t + 1) * T)
                            ch = j * 8 + hd
                            P.op("dve", lambda e, a=a, j=j, t=t, ch=ch: e.tensor_scalar(a, raw[:, j, t * T:t * T + T], cvw[:, ch, 0:1], None, ALU.mult),
                                 reads=[Rraw, Rc], writes=[ra])
                            for tap in range(1, 5):
                                P.op("dve", lambda e, a=a, j=j, t=t, ch=ch, tap=tap: e.scalar_tensor_tensor(
                                    a, raw[:, j, t * T + tap:t * T + tap + T], cvw[:, ch, tap:tap + 1], a, ALU.mult, ALU.add),
                                    reads=[Rraw, Rc, ra], writes=[ra])
                            if j == 2:
                                P.op("act", lambda e, a=a, tsl=tsl: e.activation(out=VT[:, tsl], in_=a, func=AF.Silu), reads=[ra], writes=[RVT])
                                continue
                            P.op("act", lambda e, a=a, so=so: e.activation(out=so, in_=a, func=AF.Silu), reads=[ra], writes=[rso])
                            P.op("pool", lambda e, so=so: e.tensor_tensor(sqb, so, so, ALU.mult), reads=[rso], writes=[Rsqb])
                            pb = nb()
                            mm(C.psum[pb], C.psr[pb], [(ones_bf, sqb)], [Rsqb, R_const])
                            rsqrt_ps(rs, Rrs, C.psum[pb], C.psr[pb], scale=float(D))
                            dstT, rdst, scl = (QT, RQ, 128.0 ** -0.5) if j == 0 else (KT, RK, 1.0)
                            P.op("dve", lambda e, so=so, dstT=dstT, tsl=tsl, scl=scl: e.scalar_tensor_tensor(
                                dstT[:, tsl], so, scl, rs, ALU.mult, ALU.mult), reads=[rso, Rrs], writes=[rdst])
                    for (srcT, rsrc, dst, rdst) in ((KT, RK, Ktm, RKt), (VT, RVT, Vtm, RVt)):
                        for n4 in range(N // 4):
                            pb = nb()
                            for q in range(4):
                                n = n4 * 4 + q
                                mm(C.psum[pb][0:64, q * 128:(q + 1) * 128], C.psr[pb], [(srcT[:, n * 64:(n + 1) * 64], I128b)], [rsrc, Rc])
                            P.op("act", lambda e, pb=pb, dst=dst, n4=n4: e.activation(
                                out=dst[0:64, n4 * 4:(n4 + 1) * 4, :], in_=C.psum[pb][0:64, :].rearrange("p (a b) -> p a b", a=4), func=AF.Copy),
                                reads=[C.psr[pb]], writes=[rdst])
                    P.barrier()
                    ar.reset(mp)
                if GDN_STAGE >= 3:
                    prep()

                def direction(d_):
                    md = ar.mark()
                    c = d_ * 8 + hd
                    NBc = 8
                    Sf = ar.alloc([128], F32)
                    Sb = ar.alloc([128], BF16)
                    RS, RSb = Res("S"), Res("Sb")
                    P.op("pool", lambda e: e.memset(Sf, 0.0), writes=[RS])
                    P.op("pool", lambda e: e.memset(Sb, 0.0), writes=[RSb])
                    bV = ar.alloc([NBc, 128], F32)
                    bgK = ar.alloc([NBc, 128], F32)
                    Kd = ar.alloc([NBc, 128], BF16)
                    ngB = ar.alloc([NBc, 64], F32)
                    RngB = Res("ngB")
                    Dbs = ar.alloc([NBc * 64], F32)
                    Dis = ar.alloc([NBc * 64], F32)
                    Nbf = [ar.alloc([NBc, 64], F32) for _ in range(2)]
                    Mb = [ar.alloc([NBc, 64], F32) for _ in range(2)]
                    Pb = [ar.alloc([NBc, 64], F32) for _ in range(2)]
                    QKD = ar.alloc([NBc, 64], BF16)
                    QKDT = ar.alloc([NBc, 64], BF16)
                    U = ar.alloc([NBc, 128], F32)
                    WT = ar.alloc([NBc, 64], F32)
                    Vn = [ar.alloc([128], BF16) for _ in range(2)]
                    o1 = [ar.alloc([128], F32) for _ in range(2)]
                    Ob = ar.alloc([NBc, 128], F32)
                    Of = ar.alloc([NBc, 128], F32)
                    osq = ar.alloc([NBc, 128], F32)
                    ors = ar.alloc([NBc], F32)
                    onb = ar.alloc([NBc, 128], BF16)
                    gt = ar.alloc([NBc * 64], BF16)
                    sgt = ar.alloc([NBc * 64], BF16)
                    oTb = ar.alloc([NBc * 64], BF16)
                    names = "bV bgK Kd Db Di QKD QKDT U WT Ob Of osq ors onb gt sgt oTb".split()
                    Rr = {n_: Res(n_) for n_ in names}
                    RM = [Res("M0"), Res("M1")]
                    RN = [Res("N0"), Res("N1")]
                    RP = [Res("P0"), Res("P1")]
                    RVn = [Res("Vn0"), Res("Vn1")]
                    Ro1 = [Res("o10"), Res("o11")]
                    NEGSt = bcast_free(NEGS[d_], NBc, 1)
                    NEGIt = bcast_free(NEGI[d_], NBc, 1)
                    blocks = list(range(N // NBc))
                    if d_ == 1:
                        blocks = blocks[::-1]
                    vcnt = 0
                    for bk in blocks:
                        n0 = bk * NBc
                        nsl = slice(n0, n0 + NBc)
                        for (dst, src, gq, rn, rs_) in ((bV, Vtm, "beta", "bV", RVt), (bgK, Ktm, "bege", "bgK", RKt), (Kd, Ktm, "ekd", "Kd", RKt)):
                            P.op("dve", lambda e, dst=dst, src=src, gq=gq, nsl=nsl: e.tensor_tensor(
                                dst[0:64], src[0:64, nsl, :], bcast_free(Gh[gq][0:64, d_, nsl], 128, 2), ALU.mult),
                                reads=[rs_, Rgh], writes=[Rr[rn]])
                        P.op("dve", lambda e, nsl=nsl: e.tensor_copy(ngB[0:64], bcast_free(Gh["ngc"][0:64, d_, nsl], 64, 2)),
                             reads=[Rgh], writes=[RngB])
                        pkk, pqk, pdb, pdi = nb(), nb(), nb(), nb()
                        for j in range(NBc):
                            n = n0 + j
                            csl = slice(n * 64, (n + 1) * 64)
                            mm(C.psum[pkk][0:64, j * 64:(j + 1) * 64], C.psr[pkk], [(KT[:, csl], KT[:, csl])], [RK])
                            mm(C.psum[pqk][0:64, j * 64:(j + 1) * 64], C.psr[pqk], [(QT[:, csl], KT[:, csl])], [RK, RQ])
                        for (pd, NEGt, gq) in ((pdb, NEGSt, "gcb"), (pdi, NEGIt, "gc")):
                            P.op("pe", lambda e, pd=pd, NEGt=NEGt: e.matmul(C.psum[pd][0:64, :].rearrange("p (a b) -> p a b", a=NBc), I64f, NEGt, start=True, stop=False),
                                 reads=[Rc], writes=[C.psr[pd]], sig=False)
                            for j in range(NBc):
                                n = n0 + j
                                last = (j == NBc - 1)
                                P.op("pe", lambda e, pd=pd, j=j, n=n, gq=gq: e.matmul(
                                    C.psum[pd][0:64, j * 64:(j + 1) * 64], I64f, bcast_free(Gh[gq][0:64, d_, n], 64, 1), start=False, stop=False),
                                    reads=[Rc, Rgh], writes=[C.psr[pd]], sig=False)
                                P.op("pe", lambda e, pd=pd, j=j, n=n, last=last: e.matmul(
                                    C.psum[pd][0:64, j * 64:(j + 1) * 64], ngB[0:64, j, :], I64f, start=False, stop=last),
                                    reads=[Rc, Rgh, RngB], writes=[C.psr[pd]], sig=last)
                        P.op("act", lambda e, pdb=pdb: e.activation(out=Dbs[0:64], in_=C.psum[pdb][0:64, :], func=AF.Exp), reads=[C.psr[pdb]], writes=[Rr["Db"]])
                        P.op("act", lambda e, pdi=pdi: e.activation(out=Dis[0:64], in_=C.psum[pdi][0:64, :], func=AF.Exp), reads=[C.psr[pdi]], writes=[Rr["Di"]])
                        P.op("dve", lambda e, pkk=pkk: e.tensor_tensor(Mb[0][0:64].rearrange("p a b -> p (a b)"), C.psum[pkk][0:64, :], Dbs[0:64], ALU.mult),
                             reads=[C.psr[pkk], Rr["Db"]], writes=[RM[0]])
                        P.op("dve", lambda e, pqk=pqk: e.tensor_tensor(QKD[0:64].rearrange("p a b -> p (a b)"), C.psum[pqk][0:64, :], Dis[0:64], ALU.mult),
                             reads=[C.psr[pqk], Rr["Di"]], writes=[Rr["QKD"]])
                        if GDN_SUB < 2:
                            continue
                        pn, pq = nb(), nb()
                        for j in range(NBc):
                            mm(C.psum[pn][0:64, j * 64:(j + 1) * 64], C.psr[pn], [(Mb[0][0:64, j, :], I64f)], [RM[0], Rc])
                            mm(C.psum[pq][0:64, j * 64:(j + 1) * 64], C.psr[pq], [(QKD[0:64, j, :], I64b[0:64, :])], [Rr["QKD"], Rc])
                        P.op("dve", lambda e, pn=pn: e.tensor_copy(Nbf[0][0:64].rearrange("p a b -> p (a b)"), C.psum[pn][0:64, :]),
                             reads=[C.psr[pn]], writes=[RN[0]])
                        P.op("dve", lambda e, pn=pn: e.scalar_tensor_tensor(Pb[0][0:64], C.psum[pn][0:64, :].rearrange("p (a b) -> p a b", a=NBc), -1.0,
                                                                              bcast_free(I64f, NBc, 1), ALU.mult, ALU.add),
                             reads=[C.psr[pn], Rc], writes=[RP[0]])
                        P.op("dve", lambda e, pq=pq: e.tensor_copy(QKDT[0:64].rearrange("p a b -> p (a b)"), C.psum[pq][0:64, :]),
                             reads=[C.psr[pq]], writes=[Rr["QKDT"]])
                        cm, cn, cp = 0, 0, 0
                        if GDN_SUB < 1.5:
                            continue
                        for lvl in range(GDN_NLV):
                            pm = nb()
                            for j in range(NBc):
                                if GDN_X in (3, 7):
                                    break
                                if GDN_X in (4, 8):
                                    mm(C.psum[pm][0:64, j * 64:(j + 1) * 64], C.psr[pm], [(Nbf[cn][0:64, j, :], I64b[0:64, :])], [RN[cn], RM[cm], Rc])
                                    continue
                                if GDN_X == 5:
                                    mm(C.psum[pm][0:64, j * 64:(j + 1) * 64], C.psr[pm], [(Mb[cm][0:64, j, :], Mb[cm][0:64, j, :])], [RN[cn], RM[cm], Rc])
                                    continue
                                if GDN_X == 6:
                                    mm(C.psum[pm][0:64, j * 64:(j + 1) * 64], C.psr[pm], [(I64b[0:64, :], Mb[cm][0:64, j, :])], [RN[cn], RM[cm], Rc])
                                    continue
                                mm(C.psum[pm][0:64, j * 64:(j + 1) * 64], C.psr[pm], [(Nbf[cn][0:64, j, :], Mb[cm][0:64, j, :])], [RN[cn], RM[cm]])
                            if lvl < 4 and GDN_X not in (2, 4, 5, 6, 7, 8):
                                pn2 = nb()
                                for j in range(NBc):
                                    mm(C.psum[pn2][0:64, j * 64:(j + 1) * 64], C.psr[pn2], [(Mb[cm][0:64, j, :], Nbf[cn][0:64, j, :])], [RN[cn], RM[cm]])
                            nm_, nn_ = 1 - cm, 1 - cn
                            if GDN_X not in (3, 8):
                                P.op("dve", lambda e, pm=pm, nm_=nm_: e.tensor_copy(Mb[nm_][0:64].rearrange("p a b -> p (a b)"), C.psum[pm][0:64, :]),
                                     reads=[C.psr[pm]], writes=[RM[nm_]])
                            if lvl < 4 and GDN_X not in (2, 4, 5, 6, 7, 8):
                                P.op("dve", lambda e, pn2=pn2, nn_=nn_: e.tensor_copy(Nbf[nn_][0:64].rearrange("p a b -> p (a b)"), C.psum[pn2][0:64, :]),
                                     reads=[C.psr[pn2]], writes=[RN[nn_]])
                                cn = nn_
                            cm = nm_
                            if GDN_X >= 1:
                                continue
                            pp = nb()
                            for j in range(NBc):
                                mm(C.psum[pp][0:64, j * 64:(j + 1) * 64], C.psr[pp],
                                   [(Mb[cm][0:64, j, :], Pb[cp][0:64, j, :]), (I64f, Pb[cp][0:64, j, :])], [RM[cm], RP[cp], Rc])
                            np_ = 1 - cp
                            P.op("dve", lambda e, pp=pp, np_=np_: e.tensor_copy(Pb[np_][0:64].rearrange("p a b -> p (a b)"), C.psum[pp][0:64, :]),
                                 reads=[C.psr[pp]], writes=[RP[np_]])
                            cp = np_
                        if GDN_SUB < 3:
                            continue
                        Pf, RPf = Pb[cp], RP[cp]
                        pu = [nb(), nb()]
                        pw = nb()
                        for j in range(NBc):
                            mm(C.psum[pu[j // 4]][0:64, (j % 4) * 128:(j % 4 + 1) * 128], C.psr[pu[j // 4]], [(Pf[0:64, j, :], bV[0:64, j, :])], [RPf, Rr["bV"]])
                            mm(C.psum[pw][:, j * 64:(j + 1) * 64], C.psr[pw], [(bgK[0:64, j, :], Pf[0:64, j, :])], [RPf, Rr["bgK"]])
                        for h2 in range(2):
                            P.op("act", lambda e, h2=h2, pu=pu: e.activation(out=U[0:64, h2 * 4:(h2 + 1) * 4, :], in_=C.psum[pu[h2]][0:64, :].rearrange("p (a b) -> p a b", a=4), func=AF.Copy),
                                 reads=[C.psr[pu[h2]]], writes=[Rr["U"]])
                        P.op("dve", lambda e, pw=pw: e.tensor_copy(WT.rearrange("p a b -> p (a b)"), C.psum[pw]), reads=[C.psr[pw]], writes=[Rr["WT"]])
                        if d_ == 1:
                            P.dma("sp", Of[0:64], ofw[s_][hd, n0 * 64:(n0 + NBc) * 64, :].rearrange("(n t) e -> t n e", t=64), reads=[C.scr_res], writes=[Rr["Of"]])
                            P.dma("sp", gt, pj[s_][3072 + hd * 128:3072 + (hd + 1) * 128, n0 * 64:(n0 + NBc) * 64], reads=[C.scr_res], writes=[Rr["gt"]])
                        if GDN_SUB < 4:
                            continue
                        js = list(range(NBc))
                        if d_ == 1:
                            js = js[::-1]
                        for j in js:
                            n = n0 + j
                            csl = slice(n * 64, (n + 1) * 64)
                            pws, pqs, po2, psn = nb(), nb(), nb(), nb()
                            vn, rvn = Vn[vcnt % 2], RVn[vcnt % 2]
                            ot, rot = o1[vcnt % 2], Ro1[vcnt % 2]
                            vcnt += 1
                            mm(C.psum[pws][0:64, 0:128], C.psr[pws], [(WT[:, j, :], Sf)], [Rr["WT"], RS])
                            mm(C.psum[pqs][0:64, 0:128], C.psr[pqs], [(QT[:, csl], Sb)], [RQ, RSb])
                            P.op("dve", lambda e, vn=vn, j=j, pws=pws: e.tensor_tensor(vn[0:64], U[0:64, j, :], C.psum[pws][0:64, 0:128], ALU.subtract),
                                 reads=[Rr["U"], C.psr[pws]], writes=[rvn])
                            mm(C.psum[po2][0:64, 0:128], C.psr[po2], [(QKDT[0:64, j, :], vn[0:64])], [Rr["QKDT"], rvn])
                            mm(C.psum[psn][:, 0:128], C.psr[psn], [(Kd[0:64, j, :], vn[0:64])], [Rr["Kd"], rvn])
                            P.op("dve", lambda e, psn=psn, n=n: e.scalar_tensor_tensor(Sb, Sf, EGLh[:, d_, n:n + 1], C.psum[psn][:, 0:128], ALU.mult, ALU.add),
                                 reads=[RS, C.psr[psn], Rgh], writes=[RSb])
                            P.op("dve", lambda e, psn=psn, n=n: e.scalar_tensor_tensor(Sf, Sf, EGLh[:, d_, n:n + 1], C.psum[psn][:, 0:128], ALU.mult, ALU.add),
                                 reads=[RS, C.psr[psn], Rgh], writes=[RS])
                            P.op("act", lambda e, ot=ot, pqs=pqs, n=n: e.activation(out=ot[0:64], in_=C.psum[pqs][0:64, 0:128], func=AF.Copy,
                                                                                  scale=Gh["egc"][0:64, d_, n:n + 1]),
                                 reads=[C.psr[pqs], Rgh], writes=[rot])
                            P.op("dve", lambda e, ot=ot, po2=po2, j=j: e.tensor_tensor(Ob[0:64, j, :], ot[0:64], C.psum[po2][0:64, 0:128], ALU.add),
                                 reads=[rot, C.psr[po2]], writes=[Rr["Ob"]])
                        if d_ == 0:
                            P.dma("pool", ofw[s_][hd, n0 * 64:(n0 + NBc) * 64, :].rearrange("(n t) e -> t n e", t=64), Ob[0:64], reads=[Rr["Ob"]], writes=[C.scr_res])
                        else:
                            P.op("pool", lambda e: e.tensor_tensor(Ob[0:64], Ob[0:64], Of[0:64], ALU.add), reads=[Rr["Ob"], Rr["Of"]], writes=[Rr["Ob"]])
                            P.op("pool", lambda e: e.tensor_tensor(osq[0:64], Ob[0:64], Ob[0:64], ALU.mult), reads=[Rr["Ob"]], writes=[Rr["osq"]])
                            P.op("dve", lambda e: e.tensor_reduce(ors[0:64], osq[0:64], AX.X, ALU.add), reads=[Rr["osq"]], writes=[Rr["ors"]])
                            P.op("act", lambda e: e.activation(out=ors[0:64], in_=ors[0:64], func=AF.Sqrt, bias=eps_col[0:64, :], scale=1.0 / 128.0),
                                 reads=[Rr["ors"], R_const], writes=[Rr["ors"]])
                            P.op("dve", lambda e: e.reciprocal(ors[0:64], ors[0:64]), reads=[Rr["ors"]], writes=[Rr["ors"]])
                            P.op("dve", lambda e: e.tensor_tensor(onb[0:64], Ob[0:64], bcast_free(ors[0:64], 128, 2), ALU.mult),
                                 reads=[Rr["Ob"], Rr["ors"]], writes=[Rr["onb"]])
                            pt_ = nb()
                            for j in range(NBc):
                                mm(C.psum[pt_][:, j * 64:(j + 1) * 64], C.psr[pt_], [(onb[0:64, j, :], I64b[0:64, :])], [Rr["onb"], Rc])
                            P.op("act", lambda e: e.activation(out=sgt, in_=gt, func=AF.Silu), reads=[Rr["gt"]], writes=[Rr["sgt"]])
                            P.op("dve", lambda e, pt_=pt_: e.scalar_tensor_tensor(oTb, C.psum[pt_], onw[:, 0:1], sgt, ALU.mult, ALU.mult),
                                 reads=[C.psr[pt_], Rr["sgt"], Rc], writes=[Rr["oTb"]])
                            P.dma("pool", oscr[s_][hd * 128:(hd + 1) * 128, n0 * 64:(n0 + NBc) * 64], oTb, reads=[Rr["oTb"]], writes=[C.scr_res])
                    P.barrier()
                    ar.reset(md)
                if GDN_STAGE >= 4:
                    direction(0)
                if GDN_STAGE >= 5:
                    direction(1)
                ar.reset(mh)
            for hd in range(8):
                head(hd)
            P.barrier()
            ar.reset(m)
        for s_ in range(nseq):
            core(s_)
        mixer_post(l, "gdn_w_out", [1, D, D], i, oscr)
    C.gdn_layer = gdn_layer

    def mix_layer(l):
        kind = l % 3
        if kind == 0:
            mla_layer(l)
        elif kind == 1:
            C.gdn_layer(l)
        else:
            C.fnet_layer(l)

    for item in plan:
        if item[0] == "ffn":
            ffn_pass(item[1], item[2])
        elif item[0] == "mix":
            mix_layer(item[1])
        else:
            raise ValueError(item)
    P.barrier()
    P.emit()
    return nc, list(C.declared.keys()), layers


def _swap_halves(w):
    h = w.shape[-1] // 2
    return np.concatenate([w[..., h:], w[..., :h]], axis=-1)


def rope_tables(Smax):
    half = 32
    pos = np.arange(Smax, dtype=np.float32)
    inv_freq = (np.float32(10000.0) ** (-np.arange(half, dtype=np.float32) / np.float32(half))).astype(np.float32)
    ang = (pos[:, None] * inv_freq[None, :]).astype(np.float32)
    cos, sin = np.cos(ang).astype(np.float32), np.sin(ang).astype(np.float32)
    cos2 = np.concatenate([cos, cos], axis=1).T
    sin2s = np.concatenate([-sin, sin], axis=1).T
    return np.ascontiguousarray(cos2), np.ascontiguousarray(sin2s)


def host_layout(inputs, core, names, layers, S_list):
    nseq = 2
    xs = ("x_prompt", "x_sample")
    cs = ("c_prompt", "c_sample")
    Smax = max(S_list)
    m = {}
    for nm in names:
        if nm.startswith("xT"):
            m[nm] = np.ascontiguousarray(inputs[xs[int(nm[2:])]][core].T)
        elif nm == "cT":
            c = np.stack([inputs[n][core] for n in cs], axis=0)
            m[nm] = np.ascontiguousarray(c.reshape(nseq, DC, 128).transpose(2, 1, 0).reshape(128, DC * nseq))
        elif nm == "w_ada":
            m[nm] = np.ascontiguousarray(inputs["w_ada"][layers])
        elif nm == "b_adaT":
            m[nm] = np.ascontiguousarray(inputs["b_ada"].reshape(DEPTH, 72, 128).transpose(2, 0, 1).reshape(128, DEPTH * 72))
        elif nm == "npreT":
            m[nm] = np.ascontiguousarray(inputs["norm_pre"].reshape(DEPTH, 3, DC, 128).transpose(3, 0, 1, 2).reshape(128, -1))
        elif nm == "npostT":
            m[nm] = np.ascontiguousarray(inputs["norm_post"].reshape(DEPTH, 3, DC, 128).transpose(3, 0, 1, 2).reshape(128, -1))
        elif nm == "mla_w_down_sw":
            m[nm] = np.ascontiguousarray(_swap_halves(inputs["mla_w_down"][:, :, 640:704]))
        elif nm == "mla_w_uq_sw":
            w = inputs["mla_w_uq"].reshape(2, 384, 8, 192)[:, :, :, 128:192]
            m[nm] = np.ascontiguousarray(_swap_halves(w).reshape(2, 384, 512))
        elif nm == "mla_qnT":
            m[nm] = np.ascontiguousarray(inputs["mla_q_norm"].reshape(2, 3, 128).transpose(2, 0, 1).reshape(128, 6))
        elif nm == "mla_kvnT":
            m[nm] = np.ascontiguousarray(inputs["mla_kv_norm"].reshape(2, 2, 128).transpose(2, 0, 1).reshape(128, 4))
        elif nm == "rope_cos2":
            m[nm] = rope_tables(Smax)[0]
        elif nm == "rope_sin2s":
            m[nm] = rope_tables(Smax)[1]
        elif nm in HOST_EXTRA:
            m[nm] = HOST_EXTRA[nm](inputs, core, S_list)
        else:
            m[nm] = inputs[nm]
    return m


HOST_EXTRA = {}


def _gdn_consts(inputs, core, S_list):
    i = np.arange(64)[:, None]; j = np.arange(64)[None, :]
    NEG = -30000.0
    tri_fw = (i <= j).astype(np.float32)
    tri_bw = (i >= j).astype(np.float32)
    negi_fw = np.where(i >= j, 0.0, NEG); negs_fw = np.where(i > j, 0.0, NEG)
    negi_bw = np.where(i <= j, 0.0, NEG); negs_bw = np.where(i < j, 0.0, NEG)
    eye = np.eye(64)
    sel_fw = np.zeros((64, 128)); sel_fw[63, :] = 1.0
    sel_bw = np.zeros((64, 128)); sel_bw[0, :] = 1.0
    return np.ascontiguousarray(np.concatenate([tri_fw, tri_bw, negi_fw, negs_fw, negi_bw, negs_bw, eye, sel_fw, sel_bw], axis=1).astype(np.float32))


HOST_EXTRA["gdn_consts"] = _gdn_consts
HOST_EXTRA["gdn_convT"] = lambda inputs, core, S_list: np.ascontiguousarray(inputs["gdn_conv"][0].reshape(5, 24, 128).transpose(2, 1, 0).reshape(128, 120))
HOST_EXTRA["gdn_ab"] = lambda inputs, core, S_list: np.ascontiguousarray(np.concatenate([inputs["gdn_a_log"][0].reshape(16), inputs["gdn_dt_bias"][0].reshape(16)])[None, :])
HOST_EXTRA["gdn_onT"] = lambda inputs, core, S_list: np.ascontiguousarray(inputs["gdn_o_norm"][0].reshape(128, 1))


def _fn_cs4(inputs, core, S_list):
    c = np.arange(128)[:, None].astype(np.float64); m_ = np.arange(128)[None, :].astype(np.float64)
    a = 2 * np.pi * c * m_ / 128.0
    Cm, Sm = np.cos(a) / np.sqrt(128.0), np.sin(a) / np.sqrt(128.0)
    return np.ascontiguousarray(np.concatenate([Cm, -Sm, -Sm, -Cm], axis=1).astype(np.float32))


def _fn_tw(N2):
    def f(inputs, core, S_list):
        S = 128 * N2
        s1 = np.arange(128, dtype=np.float64)[:, None, None]
        s2 = np.arange(N2, dtype=np.float64)[None, :, None]
        k1 = np.arange(128, dtype=np.float64)[None, None, :]
        a = 2 * np.pi * np.mod(k1 * (N2 * s1 + s2), S) / S
        tw = np.stack([np.cos(a), np.sin(a)], axis=2) / np.sqrt(128.0)
        return np.ascontiguousarray(tw.reshape(128, N2 * 256).astype(np.float32))
    return f


def _fn_w(N2):
    def f(inputs, core, S_list):
        s2 = np.arange(N2, dtype=np.float64)[:, None]; k2 = np.arange(N2, dtype=np.float64)[None, :]
        a = 2 * np.pi * np.mod(s2 * k2, N2) / N2
        return np.ascontiguousarray((np.concatenate([np.cos(a), np.sin(a)], axis=0) / np.sqrt(N2)).astype(np.float32))
    return f


HOST_EXTRA["fn_cs4"] = _fn_cs4
for _n2 in (4, 8, 16, 32, 64):
    HOST_EXTRA["fn_tw%d" % _n2] = _fn_tw(_n2)
    HOST_EXTRA["fn_w%d" % _n2] = _fn_w(_n2)
HOST_EXTRA["fnet_bT"] = lambda inputs, core, S_list: np.ascontiguousarray(inputs["fnet_b_out"][0].reshape(DC, 128).T)

FULL_PLAN = []
for _l in range(DEPTH):
    FULL_PLAN += [("ffn", _l, 0), ("mix", _l), ("ffn", _l, 2)]


def run(inputs, plan, n_cores=N_CORES):
    S_list = [inputs["x_prompt"].shape[1], inputs["x_sample"].shape[1]]
    nc, names, layers = build(S_list, plan)
    in_maps = [host_layout(inputs, c, names, layers, S_list) for c in range(n_cores)]
    res = run_bass_kernel_spmd(nc, in_maps, core_ids=list(range(n_cores)))
    if DEBUG:
        LAST["res"] = res.results
    outs = []
    for s in range(2):
        outs.append(np.stack([np.ascontiguousarray(res.results[c]["yT%d" % s].T) for c in range(n_cores)], axis=0))
    return tuple(outs)


def kernel(**inputs):
    inputs = {k: np.asarray(v) for k, v in inputs.items()}
    return run(inputs, FULL_PLAN)
```
